# Optimizing a Trainium2 kernel written in Bass

```python
import math
import jax, jax.numpy as jnp
from jax import lax
import numpy as np

D_MODEL = 1024
BATCH = 2
SEQ = 8192
DEPTH = 4

CHUNK = 64
N_MIXERS = 3
N_SB_LAYERS = (DEPTH + 2) // 3
N_S5_LAYERS = (DEPTH + 1) // 3
N_CV_LAYERS = DEPTH // 3
SB_HEADS = 16
SB_HEAD_DIM = D_MODEL // SB_HEADS
Q_BLOCK = 128
S5_GROUP = 16
S5_GROUPS = D_MODEL // S5_GROUP
S5_STATE = 64
S5_DT_MIN = 1e-3
S5_DT_MAX = 1e-1
CONV_WIDTH = 31
D_FF = ((8 * D_MODEL + 2) // 3 + 255) // 256 * 256
EPS = 1e-6

kernel_name = "hybrid_stickbreak_s5_conformer_trunk"


def rms_norm(x, g):
    xf = x.astype(jnp.float32)
    y = xf * lax.rsqrt(jnp.mean(xf * xf, axis=-1, keepdims=True) + EPS)
    return (y * g.astype(jnp.float32)).astype(x.dtype)


def modulate(h, shift, scale):
    return h * (1 + scale[:, None, :]) + shift[:, None, :]


def stick_breaking_attention(u, w_qkv, w_o):
    bsz, seq, _ = u.shape
    q, k, v = jnp.split(u @ w_qkv, 3, axis=-1)
    to_heads = lambda t: t.reshape(bsz, seq, SB_HEADS, SB_HEAD_DIM).transpose(0, 2, 1, 3)
    q, k, v = to_heads(q), to_heads(k), to_heads(v)
    scale = SB_HEAD_DIM ** -0.5
    outs = []
    for blk in range(seq // Q_BLOCK):
        q0, q1 = blk * Q_BLOCK, (blk + 1) * Q_BLOCK
        qb, kb, vb = q[:, :, q0:q1], k[:, :, :q1], v[:, :, :q1]
        z = jnp.einsum('bhqd,bhkd->bhqk', qb, kb).astype(jnp.float32) * scale
        t_idx = q0 + jnp.arange(Q_BLOCK)[:, None]
        s_idx = jnp.arange(q1)[None, :]
        mask = s_idx < t_idx
        log_beta = jax.nn.log_sigmoid(z)
        log_keep = jnp.where(mask, jax.nn.log_sigmoid(-z), 0.0)
        log_later = lax.cumsum(log_keep, axis=3, reverse=True) - log_keep
        w = jnp.where(mask, jnp.exp(log_beta + log_later), 0.0)
        outs.append(jnp.einsum('bhqk,bhkd->bhqd', w.astype(vb.dtype), vb))
    o = jnp.concatenate(outs, axis=2).transpose(0, 2, 1, 3).reshape(bsz, seq, D_MODEL)
    return o @ w_o


def s5_layer(u, lam_re, lam_im, log_dt, b_re, b_im, c_re, c_im, d_skip, w_glu, b_glu):
    bsz, seq, _ = u.shape
    uf = u.astype(jnp.float32)
    ug = uf.reshape(bsz, seq, S5_GROUPS, S5_GROUP)
    dt = jnp.exp(log_dt.astype(jnp.float32))[:, None]
    lr, li = lam_re.astype(jnp.float32), lam_im.astype(jnp.float32)
    mag = jnp.exp(lr * dt)
    ar, ai = mag * jnp.cos(li * dt), mag * jnp.sin(li * dt)
    den = lr * lr + li * li
    er = ((ar - 1) * lr + ai * li) / den
    ei = (ai * lr - (ar - 1) * li) / den
    br, bi = b_re.astype(jnp.float32), b_im.astype(jnp.float32)
    bbr = er[..., None] * br - ei[..., None] * bi
    bbi = er[..., None] * bi + ei[..., None] * br
    bu_r = jnp.einsum('bsgc,gpc->bsgp', ug, bbr)
    bu_i = jnp.einsum('bsgc,gpc->bsgp', ug, bbi)
    a_r = jnp.broadcast_to(ar, (1, seq) + ar.shape)
    a_i = jnp.broadcast_to(ai, (1, seq) + ai.shape)

    def combine(e1, e2):
        a1r, a1i, b1r, b1i = e1
        a2r, a2i, b2r, b2i = e2
        return (a1r * a2r - a1i * a2i,
                a1r * a2i + a1i * a2r,
                a2r * b1r - a2i * b1i + b2r,
                a2r * b1i + a2i * b1r + b2i)

    _, _, xr, xi = lax.associative_scan(combine, (a_r, a_i, bu_r, bu_i), axis=1)
    y = (jnp.einsum('bsgp,gcp->bsgc', xr, c_re.astype(jnp.float32))
         - jnp.einsum('bsgp,gcp->bsgc', xi, c_im.astype(jnp.float32)))
    y = y.reshape(bsz, seq, D_MODEL) + d_skip.astype(jnp.float32) * uf
    y = jax.nn.gelu(y).astype(u.dtype)
    ga, gb = jnp.split(y @ w_glu + b_glu, 2, axis=-1)
    return ga * jax.nn.sigmoid(gb)


def conformer_conv(u, w_pw1, b_pw1, w_dw, b_dw, ln_g, ln_b, w_pw2, b_pw2):
    ga, gb = jnp.split(u @ w_pw1 + b_pw1, 2, axis=-1)
    h = ga * jax.nn.sigmoid(gb)
    hp = jnp.pad(h, ((0, 0), (CONV_WIDTH - 1, 0), (0, 0)))
    h = lax.conv_general_dilated(hp, w_dw[:, None, :], window_strides=(1,), padding='VALID',
                                 dimension_numbers=('NWC', 'WIO', 'NWC'),
                                 feature_group_count=D_MODEL) + b_dw
    hf = h.astype(jnp.float32)
    mu = jnp.mean(hf, axis=-1, keepdims=True)
    var = jnp.mean(jnp.square(hf - mu), axis=-1, keepdims=True)
    h = ((hf - mu) * lax.rsqrt(var + EPS) * ln_g.astype(jnp.float32)
         + ln_b.astype(jnp.float32)).astype(u.dtype)
    h = jax.nn.silu(h)
    return h @ w_pw2 + b_pw2


def swiglu(u, w_gate, w_up, w_down):
    return (jax.nn.silu(u @ w_gate) * (u @ w_up)) @ w_down


def setup_inputs(seed: int = 0) -> dict:
    key = jax.random.key(seed)
    keys = list(jax.random.split(key, 40))
    nk = lambda: keys.pop()
    nrm = lambda shape, std: jax.random.normal(nk(), shape, jnp.float32) * std
    D, F, G, P, GC = D_MODEL, D_FF, S5_GROUPS, S5_STATE, S5_GROUP
    NA, NB, NC = N_SB_LAYERS, N_S5_LAYERS, N_CV_LAYERS
    n = jnp.arange(P, dtype=jnp.float32)
    return {
        "x": nrm((BATCH, SEQ, D), 1.0),
        "c": nrm((BATCH, D), 1.0),
        "norm_g": 1.0 + nrm((DEPTH, 4, D), 0.05),
        "w_mod": nrm((DEPTH, D, 6 * D), 0.5 * D ** -0.5),
        "b_mod": nrm((DEPTH, 6 * D), 0.01),
        "sb_w_qkv": nrm((NA, D, 3 * D), D ** -0.5),
        "sb_w_o": nrm((NA, D, D), D ** -0.5),
        "s5_lam_re": -0.5 + nrm((NB, G, P), 0.01),
        "s5_lam_im": jnp.pi * n + nrm((NB, G, P), 0.01),
        "s5_log_dt": jax.random.uniform(nk(), (NB, G), jnp.float32,
                                        minval=math.log(S5_DT_MIN), maxval=math.log(S5_DT_MAX)),
        "s5_b_re": nrm((NB, G, P, GC), (2 * GC) ** -0.5),
        "s5_b_im": nrm((NB, G, P, GC), (2 * GC) ** -0.5),
        "s5_c_re": nrm((NB, G, GC, P), (2 * P) ** -0.5),
        "s5_c_im": nrm((NB, G, GC, P), (2 * P) ** -0.5),
        "s5_d": nrm((NB, D), 1.0),
        "s5_w_glu": nrm((NB, D, 2 * D), D ** -0.5),
        "s5_b_glu": nrm((NB, 2 * D), 0.01),
        "cv_w_pw1": nrm((NC, D, 2 * D), D ** -0.5),
        "cv_b_pw1": nrm((NC, 2 * D), 0.01),
        "cv_w_dw": nrm((NC, CONV_WIDTH, D), CONV_WIDTH ** -0.5),
        "cv_b_dw": nrm((NC, D), 0.01),
        "cv_ln_g": 1.0 + nrm((NC, D), 0.05),
        "cv_ln_b": nrm((NC, D), 0.01),
        "cv_w_pw2": nrm((NC, D, D), D ** -0.5),
        "cv_b_pw2": nrm((NC, D), 0.01),
        "ffn_w_gate": nrm((DEPTH, D, F), D ** -0.5),
        "ffn_w_up": nrm((DEPTH, D, F), D ** -0.5),
        "ffn_w_down": nrm((DEPTH, F, D), F ** -0.5),
    }


def reference(x, c, norm_g, w_mod, b_mod, sb_w_qkv, sb_w_o,
              s5_lam_re, s5_lam_im, s5_log_dt, s5_b_re, s5_b_im, s5_c_re, s5_c_im,
              s5_d, s5_w_glu, s5_b_glu,
              cv_w_pw1, cv_b_pw1, cv_w_dw, cv_b_dw, cv_ln_g, cv_ln_b, cv_w_pw2, cv_b_pw2,
              ffn_w_gate, ffn_w_up, ffn_w_down):
    mod_all = jnp.einsum('bd,lde->lbe', jax.nn.silu(c), w_mod) + b_mod[:, None, :]
    h = x
    for layer in range(DEPTH):
        sh_m, sc_m, g_m, sh_f, sc_f, g_f = jnp.split(mod_all[layer], 6, axis=-1)
        kind, j = layer % N_MIXERS, layer // N_MIXERS
        u = modulate(rms_norm(h, norm_g[layer, 0]), sh_m, sc_m)
        if kind == 0:
            m = stick_breaking_attention(u, sb_w_qkv[j], sb_w_o[j])
        elif kind == 1:
            m = s5_layer(u, s5_lam_re[j], s5_lam_im[j], s5_log_dt[j], s5_b_re[j], s5_b_im[j],
                         s5_c_re[j], s5_c_im[j], s5_d[j], s5_w_glu[j], s5_b_glu[j])
        else:
            m = conformer_conv(u, cv_w_pw1[j], cv_b_pw1[j], cv_w_dw[j], cv_b_dw[j],
                               cv_ln_g[j], cv_ln_b[j], cv_w_pw2[j], cv_b_pw2[j])
        h = h + g_m[:, None, :] * rms_norm(m, norm_g[layer, 1])
        u = modulate(rms_norm(h, norm_g[layer, 2]), sh_f, sc_f)
        f = swiglu(u, ffn_w_gate[layer], ffn_w_up[layer], ffn_w_down[layer])
        h = h + g_f[:, None, :] * rms_norm(f, norm_g[layer, 3])
    return h
```

```python
import contextlib
import numpy as np
import concourse.bass as bass
import concourse.mybir as mybir
from concourse.bass_utils import run_bass_kernel_spmd

F32 = mybir.dt.float32
BF16 = mybir.dt.bfloat16
F32R = mybir.dt.float32r
I32 = mybir.dt.int32
AF = mybir.ActivationFunctionType
ALU = mybir.AluOpType
AX = mybir.AxisListType

D = 1024
NCH = 8
S = 8192
B = 2
DEPTH = 4
NH = 16
HD = 64
FF = 2816
NFC = 22
T = 2048
NBLK = 16
EPS = 1e-6
NCORES = 8

ENGS = ("pe", "act", "dve", "pool", "sp")


class Prog:
    def __init__(self):
        self.nc = bass.Bass("TRN2", target_bir_lowering=False)
        self.st = contextlib.ExitStack()
        self.ops = []
        self.last_w = {}
        self.readers = {}
        self.ndma = {e: 0 for e in ENGS}
        self.KDMA = 8
        self.uid = 0

    def dram_in(self, name, shape, dt=F32):
        return self.nc.dram_tensor(name, list(shape), dt, kind="ExternalInput").ap()

    def dram_out(self, name, shape, dt=F32):
        return self.nc.dram_tensor(name, list(shape), dt, kind="ExternalOutput").ap()

    def dram_scratch(self, name, shape, dt=F32):
        return self.nc.dram_tensor(name, list(shape), dt)

    def cc(self, kind, groups, in_t, out_t, reads=(), writes=()):
        i = self.op("pool", lambda e: e.collective_compute(kind, ALU.bypass, replica_groups=groups,
                                                             ins=[in_t.ap().opt()], outs=[out_t.ap().opt()]),
                    reads, writes)
        self.ops[i]["cc"] = True
        return i

    def sb(self, name, shape, dt):
        return self.st.enter_context(self.nc.sbuf_tensor(name, list(shape), dt))

    def psum(self, name, shape=(128, 512), dt=F32):
        return self.st.enter_context(self.nc.psum_tensor(name, list(shape), dt))

    def op(self, eng, fn, reads=(), writes=(), dma=False):
        i = len(self.ops)
        deps = set()
        for k in reads:
            w = self.last_w.get(k)
            if w is not None:
                deps.add(w)
        for k in writes:
            w = self.last_w.get(k)
            if w is not None:
                deps.add(w)
            for r in self.readers.get(k, ()):
                deps.add(r)
        deps.discard(i)
        self.ops.append(dict(eng=eng, fn=fn, deps=deps, dma=dma))
        for k in reads:
            self.readers.setdefault(k, []).append(i)
        for k in writes:
            self.last_w[k] = i
            self.readers[k] = []
        return i

    def dma_bg(self, out, in_, reads=(), writes=(), **kw):
        i = self.op("pool", lambda e: e.dma_start(out=out, in_=in_, **kw), reads, writes, dma=True)
        self.ops[i]["bg"] = True
        return i

    def dma(self, eng, out, in_, reads=(), writes=(), **kw):
        return self.op(eng, lambda e: e.dma_start(out=out, in_=in_, **kw), reads, writes, dma=True)

    def finish(self):
        nc = self.nc
        ops = self.ops
        n = len(ops)
        signal = [False] * n
        for i, o in enumerate(ops):
            for d in o["deps"]:
                if not ops[d]["dma"]:
                    if ops[d]["eng"] == o["eng"] and o["eng"] == "pe" and not o["dma"]:
                        continue
                    signal[d] = True
        esem = {e: self.st.enter_context(nc.semaphore("es_" + e)) for e in ENGS}
        dsem = {e: [self.st.enter_context(nc.semaphore(f"ds_{e}_{k}")) for k in range(self.KDMA)]
                for e in ("sp", "pool", "act")}
        cnt = {e: 0 for e in ENGS}
        sigval = [0] * n
        dcount = {e: 0 for e in ENGS}
        dslot = [None] * n
        bgsem = {}
        last_op_of = {}
        for i, o in enumerate(ops):
            e = o["eng"]
            last_op_of[e] = i
            if o["dma"] and o.get("bg"):
                bgsem[i] = self.st.enter_context(nc.semaphore(f"bg_{i}"))
            elif o["dma"]:
                k = dcount[e]
                dcount[e] += 1
                dslot[i] = (e, k % self.KDMA, 16 * (k // self.KDMA + 1), k)
        for e, i in last_op_of.items():
            if not ops[i]["dma"]:
                signal[i] = True
        for i, o in enumerate(ops):
            if o.get("cc"):
                signal[i] = True
        for i, o in enumerate(ops):
            if not o["dma"] and signal[i]:
                cnt[o["eng"]] += 1
                sigval[i] = cnt[o["eng"]]
        streams = {e: [] for e in ENGS}
        for i, o in enumerate(ops):
            streams[o["eng"]].append(i)
        waited = {e: {} for e in ENGS}
        dma_by_k = {e: {} for e in ENGS}

        def emit_stream(e, eng):
            wt = waited[e]

            def wait(sem, key, val):
                if wt.get(key, 0) >= val:
                    return
                wt[key] = val
                eng.wait_ge(sem, val)

            for i in streams[e]:
                o = ops[i]
                for d in sorted(o["deps"]):
                    od = ops[d]
                    if od["dma"] and od.get("bg"):
                        wait(bgsem[d], ("bg", d), 16)
                    elif od["dma"]:
                        de, slot, val, _ = dslot[d]
                        wait(dsem[de][slot], ("d", de, slot), val)
                    else:
                        if od["eng"] == e and e == "pe" and not o["dma"]:
                            continue
                        wait(esem[od["eng"]], ("e", od["eng"]), sigval[d])
                if o["dma"] and o.get("bg"):
                    o["fn"](eng).then_inc(bgsem[i], 16)
                elif o["dma"]:
                    de, slot, val, k = dslot[i]
                    if k >= self.KDMA:
                        wait(dsem[de][slot], ("d", de, slot), val - 16)
                    o["fn"](eng).then_inc(dsem[de][slot], 16)
                else:
                    ins = o["fn"](eng)
                    if signal[i]:
                        ins.then_inc(esem[e], 1)
                    if o.get("cc"):
                        wait(esem[e], ("e", e), sigval[i])
            if e == "pool":
                for bi in sorted(bgsem):
                    wait(bgsem[bi], ("bg", bi), 16)
            if e in dsem:
                nd = dcount[e]
                for slot in range(self.KDMA):
                    ks = [k for k in range(nd) if k % self.KDMA == slot]
                    if ks:
                        wait(dsem[e][slot], ("d", e, slot), 16 * (ks[-1] // self.KDMA + 1))
            if e == "sp":
                for e2 in ENGS:
                    if e2 != "sp" and cnt[e2] > 0:
                        wait(esem[e2], ("e", e2), cnt[e2])

        blk = self.st.enter_context(nc.Block())

        @blk.tensor
        def _(eng):
            emit_stream("pe", eng)

        @blk.scalar
        def _(eng):
            emit_stream("act", eng)

        @blk.vector
        def _(eng):
            emit_stream("dve", eng)

        @blk.gpsimd
        def _(eng):
            emit_stream("pool", eng)

        @blk.sync
        def _(eng):
            emit_stream("sp", eng)

        self.st.close()
        return nc

    def barrier(self):
        self.uid += 1
        n = self.uid
        if not hasattr(self, "_bar"):
            self._bar = self.sb("bar_t", [128, 16], F32)
            b0 = self._bar
            self.op("pool", lambda en: en.memset(b0[:, 8:16], 0.0), writes=["bar_t0"])
        bt = self._bar

        def tiny(en, k, e):
            if e == "act":
                return en.activation(out=bt[:, k:k + 1], in_=bt[:, 8:9], func=AF.Copy)
            return en.memset(bt[:, k:k + 1], 0.0)

        comp = ("pe", "act", "dve", "pool")
        alldma = [i for i, o in enumerate(self.ops) if o["dma"] and i >= getattr(self, "_bar_from", 0)]
        idx = {"act": 1, "dve": 2, "pool": 3}
        first = []
        for e in ("act", "dve", "pool"):
            k = idx[e]
            i = self.op(e, lambda en, k=k, e=e: tiny(en, k, e), reads=["bar_t0"], writes=[("bar", n, e)])
            first.append(i)
        last_pe = max([i for i, o in enumerate(self.ops) if o["eng"] == "pe"], default=None)
        for e in ("act", "dve", "pool"):
            k = idx[e] + 4
            i = self.op(e, lambda en, k=k, e=e: tiny(en, k, e),
                        reads=["bar_t0"] + [("bar", n, x) for x in ("act", "dve", "pool")], writes=[("bar2", n, e)])
            self.ops[i]["deps"].update(alldma)
            if last_pe is not None:
                self.ops[i]["deps"].add(last_pe)
        self._bar_from = len(self.ops)
        self.bar_keys = [("bar2", n, e) for e in ("act", "dve", "pool")]

    def opb(self, eng, fn, reads=(), writes=(), dma=False):
        return self.op(eng, fn, list(reads) + list(getattr(self, "bar_keys", [])), writes, dma)


def _wsrc(w):
    if isinstance(w, tuple):
        return w[0], list(w[1]), "sp"
    return w, [], "pool"


def _chunks(n, sz):
    return [(i, min(sz, n - i)) for i in range(0, n, sz)]


class Kern:
    ARENA = 26624
    ARENAR = 4096

    def __init__(self, need_h=True, arena=None):
        P = self.P = Prog()
        if arena is not None:
            self.ARENA = arena
        if need_h:
            self.hT = P.sb("hT", [128, NCH, T], F32)
        self.arena = P.sb("arena", [128, self.ARENA], F32)
        self.arenaR = P.sb("arenaR", [128, self.ARENAR], F32R)
        self.rpos = 0
        self.ones_bf = P.sb("ones_bf", [128, 128], BF16)
        self.mv = P.sb("mv", [128, 48], F32)
        self.vecs = P.sb("vecs", [128, 64], F32)
        self.bank = [P.psum(f"bank{i}") for i in range(8)]
        P.op("pool", lambda e: e.memset(self.ones_bf[:], 1.0), writes=["ones_bf"])
        self.apos = 0

    def wbf(self, name, w_ap, R, C):
        P = self.P
        scr = P.dram_scratch("wbf_" + name, [R, C], BF16)
        a = 1
        while C // a > 2048 or C % a:
            a += 1
        src = w_ap.rearrange("r (a b) -> (r a) b", a=a)
        dst = scr.ap().rearrange("r (a b) -> (r a) b", a=a)
        rows = R * a
        step = 2048
        keys = []
        for r0 in range(0, rows, step):
            k = ("wbf", name, r0)
            P.dma_bg(dst[r0:min(rows, r0 + step), :], src[r0:min(rows, r0 + step), :], writes=[k])
            keys.append(k)
        return scr.ap(), keys

    def reset_arena(self):
        self.P.barrier()
        self.apos = 0

    def carveR(self, shape):
        n = int(np.prod(shape[1:]))
        off = self.rpos
        self.rpos += n
        assert self.rpos <= self.ARENAR
        return self.arenaR[0:shape[0], off:off + n]

    def carve(self, shape, dt):
        assert dt != F32R
        esz = {F32: 4, F32R: 4, BF16: 2, I32: 4}[dt]
        n = int(np.prod(shape[1:]))
        words = (n * esz + 3) // 4
        words = (words + 7) // 8 * 8
        off = self.apos
        self.apos += words
        assert self.apos <= self.ARENA, ("arena overflow", self.apos)
        h = self.arena if dt == F32 else self.arena.bitcast(dt)
        mul = 4 // esz
        ap = h[0:shape[0], off * mul: off * mul + n]
        if len(shape) == 3:
            ap = ap.rearrange("p (a b) -> p a b", a=shape[1])
        elif len(shape) == 4:
            ap = ap.rearrange("p (a b c) -> p a b c", a=shape[1], b=shape[2])
        return ap

    def load_h(self, h_in):
        P = self.P
        for c in range(NCH):
            P.dma("sp", self.hT[:, c, :], h_in[:, c, :], writes=[("h", c, tt) for tt in range(4)])

    def store_h(self, h_out):
        P = self.P
        for c in range(NCH):
            P.dma("sp", h_out[:, c, :], self.hT[:, c, :], reads=[("h", c, tt) for tt in range(4)])

    def mod_stage(self, cT, wmod, bmodT, ngT, target=None, tkey="mv"):
        P = self.P
        self.reset_arena()
        c_sb = self.carve([128, 8], F32)
        sc = self.carve([128, 8], F32)
        bm = self.carve([128, 48], F32)
        ng = self.carve([128, 32], F32)
        mo = self.carve([128, 48], F32)
        t1 = self.carve([128, 16], F32)
        wb = [self.carve([128, 8, 512], F32) for _ in range(2)]
        P.dma("sp", c_sb, cT, writes=["c_sb"], reads=P.bar_keys)
        P.dma("sp", bm, bmodT, writes=["bm"], reads=P.bar_keys)
        P.dma("sp", ng, ngT, writes=["ng"], reads=P.bar_keys)
        P.op("act", lambda e: e.activation(out=sc, in_=c_sb, func=AF.Silu), reads=["c_sb"], writes=["sc"])
        wv = wmod.rearrange("(c p) e -> p c e", p=128)
        pb = self.bank[0]
        for pc in range(12):
            w = wb[pc % 2]
            P.dma("sp", w, wv[:, :, pc * 512:(pc + 1) * 512], writes=[("wmod", pc % 2)], reads=P.bar_keys)
            for cc in range(4):
                col = pc * 4 + cc
                for k in range(8):
                    P.op("pe", lambda e, w=w, cc=cc, k=k, col=col: e.matmul(
                        pb[:, col:col + 1], w[:, k, cc * 128:(cc + 1) * 128], sc[:, k:k + 1],
                        start=(k == 0), stop=(k == 7)),
                        reads=[("wmod", pc % 2), "sc"] + P.bar_keys, writes=[("bank", 0)])
        P.op("dve", lambda e: e.tensor_tensor(out=mo, in0=pb[:, 0:48], in1=bm, op=ALU.add),
             reads=[("bank", 0), "bm"], writes=["mo"])
        mv = self.mv if target is None else target
        P.op("dve", lambda e: e.tensor_scalar(out=t1[:, 0:8], in0=mo[:, 8:16], scalar1=1.0, scalar2=None, op0=ALU.add),
             reads=["mo"], writes=["t1a"])
        P.op("dve", lambda e: e.tensor_scalar(out=t1[:, 8:16], in0=mo[:, 32:40], scalar1=1.0, scalar2=None, op0=ALU.add),
             reads=["mo"], writes=["t1b"])
        P.op("dve", lambda e: e.tensor_tensor(out=mv[:, 0:8], in0=t1[:, 0:8], in1=ng[:, 0:8], op=ALU.mult),
             reads=["t1a", "ng"], writes=[tkey])
        P.op("dve", lambda e: e.tensor_copy(out=mv[:, 8:16], in_=mo[:, 0:8]), reads=["mo"], writes=[tkey])
        P.op("dve", lambda e: e.tensor_tensor(out=mv[:, 16:24], in0=mo[:, 16:24], in1=ng[:, 8:16], op=ALU.mult),
             reads=["mo", "ng"], writes=[tkey])
        P.op("dve", lambda e: e.tensor_tensor(out=mv[:, 24:32], in0=t1[:, 8:16], in1=ng[:, 16:24], op=ALU.mult),
             reads=["t1b", "ng"], writes=[tkey])
        P.op("dve", lambda e: e.tensor_copy(out=mv[:, 32:40], in_=mo[:, 24:32]), reads=["mo"], writes=[tkey])
        P.op("dve", lambda e: e.tensor_tensor(out=mv[:, 40:48], in0=mo[:, 40:48], in1=ng[:, 24:32], op=ALU.mult),
             reads=["mo", "ng"], writes=[tkey])

    def rstd_of(self, src, srckeys, sq, lnv, rstd, bank_i, tag, N=512):
        P = self.P
        pb = self.bank[bank_i]
        P.op("act", lambda e: e.activation(out=sq, in_=src, func=AF.Square),
             reads=list(srckeys), writes=[("sq", tag)])
        for c in range(NCH):
            P.op("pe", lambda e, c=c: e.matmul(pb[:, 0:N], self.ones_bf[:], sq[:, c, :], start=(c == 0), stop=(c == 7)),
                 reads=[("sq", tag), "ones_bf"], writes=[("bank", bank_i)])
        P.op("act", lambda e: e.activation(out=lnv, in_=pb[:, 0:N], func=AF.Ln, scale=1.0 / D, bias=self.eps_ap),
             reads=[("bank", bank_i)], writes=[("lnv", tag)])
        P.op("act", lambda e: e.activation(out=rstd, in_=lnv, func=AF.Exp, scale=-0.5),
             reads=[("lnv", tag)], writes=[("rstd", tag)])

    def setup_eps(self):
        P = self.P
        self.eps_ap = self.vecs[:, 0:1]
        P.op("pool", lambda e: e.memset(self.vecs[:, 0:1], EPS), writes=["eps"])
        self.one_ap = self.vecs[:, 1:2]
        P.op("pool", lambda e: e.memset(self.vecs[:, 1:2], 1.0), writes=["one"])

    def norm_mod(self, tt, col_geff, col_shift, uT, sq, lnv, rstd, tmp2, bank_i):
        P = self.P
        mv = self.mv
        hk = [("h", c, tt) for c in range(NCH)]
        self.rstd_of(self.hT[:, :, tt * 512:(tt + 1) * 512], hk + ["eps"], sq, lnv, rstd, bank_i, "x")
        for c in range(NCH):
            tmp = tmp2[c % 2]
            P.op("dve", lambda e, c=c, tmp=tmp: e.tensor_tensor(
                out=tmp, in0=self.hT[:, c, tt * 512:(tt + 1) * 512], in1=rstd, op=ALU.mult),
                reads=[("h", c, tt), ("rstd", "x")], writes=[("tmp", c % 2)])
            P.op("act", lambda e, c=c, tmp=tmp: e.activation(
                out=uT[:, c, :], in_=tmp, func=AF.Identity,
                scale=mv[:, col_geff + c:col_geff + c + 1], bias=mv[:, col_shift + c:col_shift + c + 1]),
                reads=[("tmp", c % 2), "mv"], writes=[("uT", c)])

    def post_norm_res(self, tt, fT, fkeys, col_gg, sq, lnv, rstd, tmp2, bank_i):
        P = self.P
        mv = self.mv
        self.rstd_of(fT, list(fkeys) + ["eps"], sq, lnv, rstd, bank_i, "x")
        for c in range(NCH):
            tmp = tmp2[c % 2]
            P.op("dve", lambda e, c=c, tmp=tmp: e.scalar_tensor_tensor(
                out=tmp, in0=fT[:, c, :], scalar=mv[:, col_gg + c:col_gg + c + 1], in1=rstd,
                op0=ALU.mult, op1=ALU.mult),
                reads=list(fkeys) + [("rstd", "x"), "mv"], writes=[("tmp", c % 2)])
            P.op("pool", lambda e, c=c, tmp=tmp: e.tensor_tensor(
                out=self.hT[:, c, tt * 512:(tt + 1) * 512], in0=self.hT[:, c, tt * 512:(tt + 1) * 512],
                in1=tmp, op=ALU.add),
                reads=[("tmp", c % 2), ("h", c, tt)], writes=[("h", c, tt)])

    def ffn_stage(self, wg, wu, wd):
        P = self.P
        self.reset_arena()
        uT = self.carve([128, 8, 512], BF16)
        aT = self.carve([128, NFC, 512], BF16)
        fT = self.carve([128, 8, 512], F32)
        sq = self.carve([128, 8, 512], BF16)
        lnv = self.carve([128, 512], F32)
        rstd = self.carve([128, 512], F32)
        tmp2 = [self.carve([128, 512], F32) for _ in range(2)]
        sg2 = [self.carve([128, 512], F32) for _ in range(2)]
        wgb = [self.carve([128, 8, 256], BF16) for _ in range(2)]
        wub = [self.carve([128, 8, 256], BF16) for _ in range(2)]
        wdb = [self.carve([128, 11, 512], BF16) for _ in range(2)]
        wg, kg, eg = _wsrc(wg); wu, ku, eu = _wsrc(wu); wd, kd, ed = _wsrc(wd)
        wgv = wg.rearrange("(c p) f -> p c f", p=128)
        wuv = wu.rearrange("(c p) f -> p c f", p=128)
        wdv = wd.rearrange("(c p) d -> p c d", p=128)
        bk = P.bar_keys
        it = 0
        dit = 0
        for tt in range(4):
            self.norm_mod(tt, 24, 32, uT, sq, lnv, rstd, tmp2, 0)
            for pc in range(11):
                sl = it % 2
                it += 1
                P.dma(eg, wgb[sl], wgv[:, :, pc * 256:(pc + 1) * 256], writes=[("wg", sl)], reads=bk + kg)
                P.dma(eu, wub[sl], wuv[:, :, pc * 256:(pc + 1) * 256], writes=[("wu", sl)], reads=bk + ku)
                for cc in range(2):
                    fc = pc * 2 + cc
                    bg, bu = 1 + (fc % 2) * 2, 2 + (fc % 2) * 2
                    for k in range(8):
                        P.op("pe", lambda e, k=k, cc=cc, sl=sl, bg=bg: e.matmul(
                            self.bank[bg][:, :], wgb[sl][:, k, cc * 128:(cc + 1) * 128], uT[:, k, :],
                            start=(k == 0), stop=(k == 7)),
                            reads=[("wg", sl), ("uT", k)], writes=[("bank", bg)])
                    for k in range(8):
                        P.op("pe", lambda e, k=k, cc=cc, sl=sl, bu=bu: e.matmul(
                            self.bank[bu][:, :], wub[sl][:, k, cc * 128:(cc + 1) * 128], uT[:, k, :],
                            start=(k == 0), stop=(k == 7)),
                            reads=[("wu", sl), ("uT", k)], writes=[("bank", bu)])
                    sg = sg2[fc % 2]
                    P.op("act", lambda e, sg=sg, bg=bg: e.activation(out=sg, in_=self.bank[bg][:, :], func=AF.Silu),
                         reads=[("bank", bg)], writes=[("sg", fc % 2)])
                    P.op("dve", lambda e, sg=sg, bu=bu, fc=fc: e.tensor_tensor(
                        out=aT[:, fc, :], in0=sg, in1=self.bank[bu][:, :], op=ALU.mult),
                        reads=[("sg", fc % 2), ("bank", bu)], writes=[("aT", fc)])
            for dp in range(2):
                for hf in range(2):
                    P.dma(ed, wdb[hf], wdv[:, hf * 11:(hf + 1) * 11, dp * 512:(dp + 1) * 512],
                          writes=[("wd", hf)], reads=bk + kd)
                for dc in range(4):
                    oc = dp * 4 + dc
                    bo = 5 + (oc % 2)
                    for f in range(NFC):
                        P.op("pe", lambda e, f=f, dc=dc, bo=bo: e.matmul(
                            self.bank[bo][:, :], wdb[f // 11][:, f % 11, dc * 128:(dc + 1) * 128], aT[:, f, :],
                            start=(f == 0), stop=(f == NFC - 1)),
                            reads=[("wd", f // 11), ("aT", f)], writes=[("bank", bo)])
                    P.op("act", lambda e, oc=oc, bo=bo: e.activation(out=fT[:, oc, :], in_=self.bank[bo][:, :], func=AF.Copy),
                         reads=[("bank", bo)], writes=[("fT", oc)])
            self.post_norm_res(tt, fT, [("fT", c) for c in range(8)], 40, sq, lnv, rstd, tmp2, 7)

    def qkv_stage(self, wqkv, qT_out, kT_out, v_out):
        P = self.P
        self.reset_arena()
        uT = self.carve([128, 8, 512], BF16)
        sq = self.carve([128, 8, 512], BF16)
        lnv = self.carve([128, 512], F32)
        rstd = self.carve([128, 512], F32)
        tmp2 = [self.carve([128, 512], F32) for _ in range(2)]
        W = self.carve([128, 8, 3072], BF16)
        sg2 = [self.carve([128, 512], BF16) for _ in range(2)]
        vs2 = [self.carve([128, 512], BF16) for _ in range(2)]
        wqkv, kq, eq = _wsrc(wqkv)
        wv = wqkv.rearrange("(c p) e -> p c e", p=128)
        bk = P.bar_keys
        for pc in range(6):
            P.dma(eq, W[:, :, pc * 512:(pc + 1) * 512], wv[:, :, pc * 512:(pc + 1) * 512],
                  writes=[("wqkv", pc)], reads=bk + kq)
        it = 0
        for tt in range(4):
            self.norm_mod(tt, 0, 8, uT, sq, lnv, rstd, tmp2, 0)
            for oc in range(16):
                b = 1 + oc % 2
                for k in range(8):
                    P.op("pe", lambda e, k=k, oc=oc, b=b: e.matmul(
                        self.bank[b][:, :], W[:, k, oc * 128:(oc + 1) * 128], uT[:, k, :],
                        start=(k == 0), stop=(k == 7)),
                        reads=[("wqkv", oc // 4), ("uT", k)], writes=[("bank", b)])
                sg = sg2[oc % 2]
                P.op("act", lambda e, sg=sg, b=b, oc=oc: e.activation(
                    out=sg, in_=self.bank[b][:, :], func=AF.Copy, scale=(0.125 if oc < 8 else 1.0)),
                    reads=[("bank", b)], writes=[("qs", oc % 2)])
                dst = qT_out[:, oc, tt * 512:(tt + 1) * 512] if oc < 8 else kT_out[:, oc - 8, tt * 512:(tt + 1) * 512]
                P.dma("sp", dst, sg, reads=[("qs", oc % 2)], writes=[("qkv_s", "qk", oc, tt)])
            for tb in range(4):
                for hf in range(2):
                    b = 3 + it % 2
                    vs = vs2[it % 2]
                    for k in range(8):
                        P.op("pe", lambda e, k=k, tb=tb, hf=hf, b=b: e.matmul(
                            self.bank[b][:, :], uT[:, k, tb * 128:(tb + 1) * 128],
                            W[:, k, 2048 + hf * 512:2048 + (hf + 1) * 512],
                            start=(k == 0), stop=(k == 7)),
                            reads=[("wqkv", 4 + hf), ("uT", k)], writes=[("bank", b)])
                    P.op("dve", lambda e, vs=vs, b=b: e.tensor_copy(out=vs, in_=self.bank[b][:, :]),
                         reads=[("bank", b)], writes=[("vs", it % 2)])
                    r0 = tt * 512 + tb * 128
                    if callable(v_out):
                        for c4 in range(4):
                            P.dma("sp", v_out(hf * 4 + c4)[r0:r0 + 128, :], vs[:, c4 * 128:(c4 + 1) * 128], reads=[("vs", it % 2)],
                                  writes=[("qkv_s", "v", tt, tb, hf, c4)])
                    else:
                        P.dma("sp", v_out[r0:r0 + 128, hf * 512:(hf + 1) * 512], vs, reads=[("vs", it % 2)], writes=[("qkv_s", "v", tt, tb, hf)])
                    it += 1

    def attn_stage(self, qT_in, kT_in, v_in, masks_in, oT_dram, k_src=None, v_src=None, src_keys=()):
        P = self.P
        self.reset_arena()
        kTc = [self.carve([128, S], BF16) for _ in range(2)]
        vc = [self.carve([128, 64, 128], BF16) for _ in range(2)]
        qTc = [self.carve([128, T], BF16) for _ in range(2)]
        self.rpos = 0
        Racc = self.carveR([128, T])
        E2 = [self.carve([128, 512], F32) for _ in range(2)]
        SP3 = [self.carve([128, 512], BF16) for _ in range(3)]
        w3 = [self.carve([128, 512], BF16) for _ in range(4)]
        mk = self.carve([128, 4, 128], F32)
        mkb = self.carve([128, 4, 128], BF16)
        negL = self.carve([128, 128], BF16)
        negO = self.carveR([128, 128])
        ostg = [self.carve([128, T], BF16)] * 2
        zs3 = [self.carve([128, 512], F32) for _ in range(3)]
        ag2 = [self.carve([128, 512], F32) for _ in range(2)]
        bk = P.bar_keys
        P.dma("sp", mk, masks_in, writes=["mk"], reads=bk)
        P.op("dve", lambda e: e.tensor_copy(out=mkb, in_=mk), reads=["mk"], writes=["mkb"])
        tmpc = self.carve([128, 128], F32)
        zer = self.carve([128, 512], F32)
        P.op("pool", lambda e: e.memset(tmpc, -1.0), writes=["tmpc"])
        P.op("pool", lambda e: e.memset(zer, 0.0), writes=["zer"])
        P.op("pool", lambda e: e.tensor_copy(out=negO, in_=tmpc), reads=["tmpc"], writes=["negO"])
        P.op("pool", lambda e: e.affine_select(out=negL, in_=tmpc, pattern=[[-1, 128]], compare_op=ALU.is_ge,
                                               fill=0.0, base=0, channel_multiplier=1),
             reads=["tmpc"], writes=["negL"])
        it = 0
        for c in range(NCH):
            sl = c % 2
            if k_src is None:
                P.dma("sp", kTc[sl], kT_in[:, c, :], writes=[("kT", sl, jj) for jj in range(4)], reads=bk)
                P.dma("sp", vc[sl], v_in[c], writes=[("v", sl, jj) for jj in range(4)], reads=bk)
            else:
                kd = kTc[sl].rearrange("p (m j s) -> p m j s", m=16, j=4)
                vd = vc[sl].rearrange("s (m j) d -> s m j d", j=4)
                ks, vs_ = k_src(c), v_src(c)
                for jj in range(4):
                    P.dma("sp", kd[:, :, jj, :], ks[:, :, jj, :], writes=[("kT", sl, jj)], reads=bk + list(src_keys))
                    P.dma("sp", vd[:, :, jj, :], vs_[:, :, jj, :], writes=[("v", sl, jj)], reads=bk + list(src_keys))
            P.dma("sp", qTc[sl], qT_in[:, c, :], writes=[("qT", sl)], reads=bk + list(src_keys))
            kT, v, qT = kTc[sl], vc[sl], qTc[sl]
            for hh in range(2):
                pb = hh * 64
                og = ostg[hh]
                for q in range(4):
                    P.op("pool", lambda e, q=q: e.tensor_copy(out=Racc[:, q * 512:(q + 1) * 512], in_=zer),
                         reads=["zer"], writes=[("R", q)])
                ostarted = [False] * 4
                tiles = []
                for kb in range(63, -1, -1):
                    mmin = max(0, -(-(kb - 3) // 4))
                    mdiag = kb // 4
                    for q in range(4):
                        c0 = max(4 * q, mmin) * 128
                        c1 = (4 * q + 4) * 128
                        if c0 >= c1:
                            continue
                        st_flag = not ostarted[q]
                        ostarted[q] = True
                        tiles.append(dict(kb=kb, q=q, c0=c0, c1=c1, N=c1 - c0, r=kb % 4, diag=(c0 <= mdiag * 128 < c1),
                                          d0=mdiag * 128 - c0, st=st_flag, o0=c0 - 4 * q * 128, idx=it))
                        it += 1
                kv_keys = [("kT", sl, 0), ("kT", sl, 1), ("kT", sl, 2), ("kT", sl, 3), ("qT", sl)]
                v_keys = [("v", sl, 0), ("v", sl, 1), ("v", sl, 2), ("v", sl, 3)]

                def stA1(t):
                    i = t["idx"]; N = t["N"]; zb = self.bank[i % 2]; E = E2[i % 2]
                    ksl = kT[pb:pb + 64, t["kb"] * 128:(t["kb"] + 1) * 128]
                    qsl = qT[pb:pb + 64, t["c0"]:t["c1"]]
                    P.op("pe", lambda e: e.matmul(zb[:, 0:N], ksl, qsl, start=True, stop=True), reads=kv_keys, writes=[("bank", i % 2)])
                    P.op("act", lambda e: e.activation(out=E[:, 0:N], in_=zb[:, 0:N], func=AF.Exp), reads=[("bank", i % 2)], writes=[("E", i % 2)])

                def stA2(t):
                    i = t["idx"]; N = t["N"]; E = E2[i % 2]; SP = SP3[i % 3]
                    P.op("act", lambda e: e.activation(out=SP[:, 0:N], in_=E[:, 0:N], func=AF.Ln, bias=self.one_ap, scale=1.0),
                         reads=[("E", i % 2), "one"], writes=[("SP", i % 3)])
                    if t["diag"]:
                        d0, r = t["d0"], t["r"]
                        P.op("dve", lambda e: e.tensor_tensor(out=SP[:, d0:d0 + 128], in0=SP[:, d0:d0 + 128], in1=mk[:, r, :], op=ALU.mult),
                             reads=[("SP", i % 3), "mk"], writes=[("SP", i % 3)])

                def stB1(t):
                    i = t["idx"]; N = t["N"]; ab = self.bank[2 + i % 2]; SP = SP3[i % 3]; q = t["q"]; c0, c1 = t["c0"], t["c1"]
                    ksl = kT[pb:pb + 64, t["kb"] * 128:(t["kb"] + 1) * 128]
                    qsl = qT[pb:pb + 64, c0:c1]
                    P.op("pe", lambda e: e.matmul(ab[:, 0:N], ksl, qsl, start=True, stop=False), reads=kv_keys, writes=[("bank", 2 + i % 2)])
                    P.op("pe", lambda e: e.matmul(ab[:, 0:N], negL, SP[:, 0:N], start=False, stop=False),
                         reads=[("SP", i % 3), "negL"], writes=[("bank", 2 + i % 2)])
                    P.op("pe", lambda e: e.matmul(ab[:, 0:N], negO, Racc[:, c0:c1], start=False, stop=True),
                         reads=[("R", q), "negO"], writes=[("bank", 2 + i % 2)])
                    P.op("pool", lambda e: e.tensor_tensor(out=Racc[:, c0:c1], in0=Racc[:, c0:c1], in1=SP[:, 0:N], op=ALU.add),
                         reads=[("SP", i % 3), ("R", q)], writes=[("R", q)])

                def stB2(t):
                    i = t["idx"]; N = t["N"]; ab = self.bank[2 + i % 2]; w = w3[i % 4]
                    P.op("act", lambda e: e.activation(out=w[:, 0:N], in_=ab[:, 0:N], func=AF.Exp), reads=[("bank", 2 + i % 2)], writes=[("w", i % 4)])
                    if t["diag"]:
                        d0, r = t["d0"], t["r"]
                        P.op("dve", lambda e: e.tensor_tensor(out=w[:, d0:d0 + 128], in0=w[:, d0:d0 + 128], in1=mkb[:, r, :], op=ALU.mult),
                             reads=[("w", i % 4), "mkb"], writes=[("w", i % 4)])

                def stC(t):
                    i = t["idx"]; N = t["N"]; w = w3[i % 4]; q = t["q"]; ob = self.bank[4 + q]; o0 = t["o0"]; kb = t["kb"]; stf = t["st"]
                    vsl = v[:, kb, pb:pb + 64]
                    P.op("pe", lambda e: e.matmul(ob[0:64, o0:o0 + N], vsl, w[:, 0:N], start=stf, stop=(kb == 0), skip_group_check=True),
                         reads=[("w", i % 4)] + v_keys, writes=[("bank", 4 + q)])

                nT = len(tiles)
                for step in range(nT + 3):
                    if step < nT:
                        stA1(tiles[step])
                    if 2 <= step <= nT + 1:
                        stB2(tiles[step - 2])
                    if step < nT:
                        stA2(tiles[step])
                    if 1 <= step <= nT:
                        stB1(tiles[step - 1])
                    if 3 <= step:
                        stC(tiles[step - 3])
                for q in range(4):
                    P.op("dve", lambda e, q=q, og=og: e.tensor_copy(out=og[0:64, q * 512:(q + 1) * 512], in_=self.bank[4 + q][0:64, :]),
                         reads=[("bank", 4 + q)], writes=[("ostg", 0)])
                P.dma("sp", oT_dram[pb:pb + 64, c, :], og[0:64, :], reads=[("ostg", 0)], writes=["oT_dram"])

    def oproj_stage(self, wo, oT_dram):
        P = self.P
        self.reset_arena()
        W = self.carve([128, 8, 1024], BF16)
        oT2 = [self.carve([128, 8, 512], BF16) for _ in range(2)]
        mT = self.carve([128, 8, 512], F32)
        sq = self.carve([128, 8, 512], BF16)
        lnv = self.carve([128, 512], F32)
        rstd = self.carve([128, 512], F32)
        tmp2 = [self.carve([128, 512], F32) for _ in range(2)]
        bk = P.bar_keys
        wo, ko, eo = _wsrc(wo)
        wv = wo.rearrange("(c p) e -> p c e", p=128)
        for pc in range(2):
            P.dma(eo, W[:, :, pc * 512:(pc + 1) * 512], wv[:, :, pc * 512:(pc + 1) * 512], writes=[("wo", pc)], reads=bk + ko)
        for tt in range(4):
            oT = oT2[tt % 2]
            P.dma("sp", oT, oT_dram[:, :, tt * 512:(tt + 1) * 512], reads=["oT_dram"] + bk, writes=[("oT", tt % 2)])
            for oc in range(8):
                b = 1 + oc % 2
                for k in range(8):
                    P.op("pe", lambda e, k=k, oc=oc, b=b, oT=oT: e.matmul(
                        self.bank[b][:, :], W[:, k, oc * 128:(oc + 1) * 128], oT[:, k, :], start=(k == 0), stop=(k == 7)),
                        reads=[("wo", oc // 4), ("oT", tt % 2)], writes=[("bank", b)])
                P.op("act", lambda e, oc=oc, b=b: e.activation(out=mT[:, oc, :], in_=self.bank[b][:, :], func=AF.Copy),
                     reads=[("bank", b)], writes=[("mT", oc)])
            self.post_norm_res(tt, mT, [("mT", c) for c in range(8)], 16, sq, lnv, rstd, tmp2, 7)

    def use_mod(self, l):
        src = self.mvs[l]
        self.P.op("dve", lambda e: e.tensor_copy(out=self.mv[:, :], in_=src[:, :]), reads=[("mvs", l)], writes=["mv"])

    def load_mv(self, mv_in):
        self.P.dma("sp", self.mv[:, :], mv_in, writes=["mv"])

    def store_mv(self, mv_out):
        self.P.dma("sp", mv_out, self.mv[:, :], reads=["mv"])


try:
    import ml_dtypes
    NP_BF16 = ml_dtypes.bfloat16
except Exception:
    NP_BF16 = None


def core_tokens(j):
    return np.concatenate([np.arange(128) + (4 * m + j) * 128 for m in range(NBLK)])


def to_fm(a):
    t = a.shape[0]
    return np.ascontiguousarray(a.T.reshape(NCH, 128, t).transpose(1, 0, 2))


def from_fm(a):
    t = a.shape[2]
    return np.ascontiguousarray(a.transpose(1, 0, 2).reshape(D, t).T)


def vec_fm(v, n):
    return np.ascontiguousarray(v.reshape(n, 128).T)


def make_masks(j):
    s = np.arange(128)[:, None]
    t = np.arange(128)[None, :]
    m = np.zeros((128, 4, 128), np.float32)
    for r in range(4):
        if r < j:
            m[:, r, :] = 1.0
        elif r == j:
            m[:, r, :] = (s < t).astype(np.float32)
    return m


def run(nc, in_maps):
    res = run_bass_kernel_spmd(nc, in_maps, core_ids=list(range(NCORES)))
    return res.results


def build_A(mod_inputs=True):
    K = Kern()
    P = K.P
    h_in = P.dram_in("h_in", [128, NCH, T])
    cT = P.dram_in("cT", [128, 8])
    wmod = P.dram_in("wmod", [D, 6 * D])
    bmodT = P.dram_in("bmodT", [128, 48])
    ngT = P.dram_in("ngT", [128, 32])
    wqkv = P.dram_in("wqkv", [D, 3 * D])
    qT_out = P.dram_out("qT_out", [128, NCH, T], BF16)
    kT_out = P.dram_out("kT_out", [128, NCH, T], BF16)
    v_out = P.dram_out("v_out", [T, D], BF16)
    mv_out = P.dram_out("mv_out", [128, 48])
    K.setup_eps()
    K.load_h(h_in)
    K.mod_stage(cT, wmod, bmodT, ngT)
    K.store_mv(mv_out)
    K.qkv_stage(wqkv, qT_out, kT_out, v_out)
    return P.finish()


def build_B():
    K = Kern()
    P = K.P
    h_in = P.dram_in("h_in", [128, NCH, T])
    mv_in = P.dram_in("mv_in", [128, 48])
    qT_in = P.dram_in("qT_in", [128, NCH, T], BF16)
    kT_in = P.dram_in("kT_in", [128, NCH, S], BF16)
    v_in = P.dram_in("v_in", [NCH, 128, 64, 128], BF16)
    masks = P.dram_in("masks", [128, 4, 128])
    wo = P.dram_in("wo", [D, D])
    wg = P.dram_in("wg", [D, FF])
    wu = P.dram_in("wu", [D, FF])
    wd = P.dram_in("wd", [FF, D])
    oT_dram = P.dram_out("oT_dram", [128, NCH, T], BF16)
    h_out = P.dram_out("h_out", [128, NCH, T])
    K.setup_eps()
    K.load_h(h_in)
    K.load_mv(mv_in)
    K.attn_stage(qT_in, kT_in, v_in, masks, oT_dram)
    K.oproj_stage(wo, oT_dram)
    K.ffn_stage(wg, wu, wd)
    K.store_h(h_out)
    return P.finish()


def gather_kv(resA):
    kT_b, v_b = [], []
    for b in range(B):
        kT = np.zeros((128, NCH, S), NP_BF16)
        v = np.zeros((S, D), NP_BF16)
        for j in range(4):
            r = resA[b * 4 + j]
            tok = core_tokens(j)
            kT[:, :, tok] = r["kT_out"]
            v[tok, :] = r["v_out"]
        v_l = np.ascontiguousarray(v.reshape(64, 128, NCH, 128).transpose(2, 1, 0, 3))
        kT_b.append(kT)
        v_b.append(v_l)
    return kT_b, v_b


def _k_u_out(K, u_out, colg=0, cols=8, u_out_bf=None):
    P = K.P
    K.reset_arena()
    sq = K.carve([128, 8, 512], BF16)
    lnv = K.carve([128, 512], F32)
    rstd = K.carve([128, 512], F32)
    tmp2 = [K.carve([128, 512], F32) for _ in range(2)]
    uo2 = [K.carve([128, 512], F32) for _ in range(2)]
    ub2_ = [K.carve([128, 512], BF16) for _ in range(2)]
    mv = K.mv
    for tt in range(4):
        hk = [("h", c, tt) for c in range(NCH)]
        K.rstd_of(K.hT[:, :, tt * 512:(tt + 1) * 512], hk + ["eps"], sq, lnv, rstd, 0, "x")
        for c in range(NCH):
            tmp, uo = tmp2[c % 2], uo2[c % 2]
            P.op("dve", lambda e, c=c, tmp=tmp, tt=tt: e.tensor_tensor(out=tmp, in0=K.hT[:, c, tt * 512:(tt + 1) * 512], in1=rstd, op=ALU.mult),
                 reads=[("h", c, tt), ("rstd", "x")], writes=[("tmp", c % 2)])
            P.op("act", lambda e, c=c, tmp=tmp, uo=uo: e.activation(out=uo, in_=tmp, func=AF.Identity,
                 scale=mv[:, colg + c:colg + c + 1], bias=mv[:, cols + c:cols + c + 1]),
                 reads=[("tmp", c % 2), "mv"], writes=[("uo", c % 2)])
            P.dma("sp", u_out[:, c, tt * 512:(tt + 1) * 512], uo, reads=[("uo", c % 2)], writes=[("u_s", c, tt)])
            if u_out_bf is not None:
                ubf = ub2_[c % 2]
                P.op("act", lambda e, c=c, tmp=tmp, ubf=ubf: e.activation(out=ubf, in_=tmp, func=AF.Identity,
                     scale=mv[:, colg + c:colg + c + 1], bias=mv[:, cols + c:cols + c + 1]),
                     reads=[("tmp", c % 2), "mv"], writes=[("uobf", c % 2)])
                P.dma("sp", u_out_bf[:, c, tt * 512:(tt + 1) * 512], ubf, reads=[("uobf", c % 2)], writes=[("u_sb", c, tt)])


def _glu_mix(K, src_fn, W_dram, bT, post_col=16):
    P = K.P
    W = K.carve([128, 8, 2048], BF16)
    mT = K.carve([128, 8, 512], F32)
    sg2 = [K.carve([128, 512], F32) for _ in range(2)]
    sq = K.carve([128, 8, 512], BF16)
    lnv = K.carve([128, 512], F32)
    rstd = K.carve([128, 512], F32)
    tmp2 = [K.carve([128, 512], F32) for _ in range(2)]
    W_dram, kw_, ew_ = _wsrc(W_dram)
    wv = W_dram.rearrange("(c p) e -> p c e", p=128)
    for pc in range(4):
        P.dma(ew_, W[:, :, pc * 512:(pc + 1) * 512], wv[:, :, pc * 512:(pc + 1) * 512], writes=[("wglu", pc)], reads=P.bar_keys + kw_)
    for tt in range(4):
        xT, xkeys = src_fn(tt)
        for oc in range(8):
            ba, bb = 1 + (oc % 2) * 2, 2 + (oc % 2) * 2
            for k in range(8):
                P.op("pe", lambda e, k=k, oc=oc, ba=ba: e.matmul(K.bank[ba][:, :], W[:, k, oc * 128:(oc + 1) * 128], xT[:, k, :], start=(k == 0), stop=(k == 7)),
                     reads=[("wglu", oc // 4)] + xkeys, writes=[("bank", ba)])
            for k in range(8):
                P.op("pe", lambda e, k=k, oc=oc, bb=bb: e.matmul(K.bank[bb][:, :], W[:, k, 1024 + oc * 128:1024 + (oc + 1) * 128], xT[:, k, :], start=(k == 0), stop=(k == 7)),
                     reads=[("wglu", 2 + oc // 4)] + xkeys, writes=[("bank", bb)])
            sg = sg2[oc % 2]
            P.op("act", lambda e, sg=sg, bb=bb, oc=oc: e.activation(out=sg, in_=K.bank[bb][:, :], func=AF.Sigmoid, bias=bT[:, 8 + oc:9 + oc], scale=1.0),
                 reads=[("bank", bb), "bT"], writes=[("sg", oc % 2)])
            P.op("dve", lambda e, sg=sg, ba=ba, oc=oc: e.scalar_tensor_tensor(out=mT[:, oc, :], in0=K.bank[ba][:, :], scalar=bT[:, oc:oc + 1], in1=sg, op0=ALU.add, op1=ALU.mult),
                 reads=[("bank", ba), ("sg", oc % 2), "bT"], writes=[("mT", oc)])
        yield tt, mT, sq, lnv, rstd, tmp2


def _k_s5_post(K, y_in, u_in, dT_in, bT_in, wglu):
    P = K.P
    K.reset_arena()
    bT = K.carve([128, 16], F32)
    dT = K.carve([128, 8], F32)
    yb2 = [K.carve([128, 512], F32) for _ in range(2)]
    ub2 = [K.carve([128, 512], F32) for _ in range(2)]
    gT = K.carve([128, 8, 512], BF16)
    t2 = [K.carve([128, 512], F32) for _ in range(2)]
    bk = P.bar_keys
    P.dma("sp", bT, bT_in, writes=["bT"], reads=bk)
    P.dma("sp", dT, dT_in, writes=["dT"], reads=bk)

    def src(tt):
        for c in range(8):
            a, b2 = t2[0], t2[1]
            yb, ub = yb2[c % 2], ub2[c % 2]
            yk, uk = ("yb", c % 2), ("ub", c % 2)
            P.dma("sp", yb, y_in[:, c, tt * 512:(tt + 1) * 512], writes=[yk], reads=bk + [("y_s", c, mm) for mm in range(tt * 4, tt * 4 + 4)])
            P.dma("sp", ub, u_in[:, c, tt * 512:(tt + 1) * 512], writes=[uk], reads=bk + [("u_s", c, tt)])
            P.op("dve", lambda e, c=c, yb=yb, ub=ub: e.scalar_tensor_tensor(out=yb, in0=ub, scalar=dT[:, c:c + 1], in1=yb, op0=ALU.mult, op1=ALU.add),
                 reads=[yk, uk, "dT"], writes=[yk])
            P.op("act", lambda e, yb=yb, a=a: e.activation(out=a, in_=yb, func=AF.Square), reads=[yk], writes=["t2a"])
            P.op("dve", lambda e, a=a: e.tensor_scalar(out=a, in0=a, scalar1=0.044715, scalar2=1.0, op0=ALU.mult, op1=ALU.add), reads=["t2a"], writes=["t2a"])
            P.op("dve", lambda e, yb=yb, a=a: e.tensor_tensor(out=a, in0=a, in1=yb, op=ALU.mult), reads=["t2a", yk], writes=["t2a"])
            P.op("act", lambda e, a=a, b2=b2: e.activation(out=b2, in_=a, func=AF.Sigmoid, scale=1.5957691216057308), reads=["t2a"], writes=["t2b"])
            P.op("dve", lambda e, c=c, b2=b2, yb=yb: e.tensor_tensor(out=gT[:, c, :], in0=b2, in1=yb, op=ALU.mult), reads=["t2b", yk], writes=[("gT", c)])
        return gT, [("gT", c) for c in range(8)]

    for tt, mT, sq, lnv, rstd, tmp2 in _glu_mix(K, src, wglu, bT):
        K.post_norm_res(tt, mT, [("mT", c) for c in range(8)], 16, sq, lnv, rstd, tmp2, 7)


def _k_conv_pre(K, w1, b1T_in, hc_out, tails=None):
    P = K.P
    K.reset_arena()
    bT = K.carve([128, 16], F32)
    uT = K.carve([128, 8, 512], BF16)
    sq0 = K.carve([128, 8, 512], BF16)
    lnv0 = K.carve([128, 512], F32)
    rstd0 = K.carve([128, 512], F32)
    tmp0 = [K.carve([128, 512], F32) for _ in range(2)]
    P.dma("sp", bT, b1T_in, writes=["bT"], reads=P.bar_keys)

    def src(tt):
        K.norm_mod(tt, 0, 8, uT, sq0, lnv0, rstd0, tmp0, 0)
        return uT, [("uT", c) for c in range(8)]

    for tt, mT, sq, lnv, rstd, tmp2 in _glu_mix(K, src, w1, bT):
        for c in range(8):
            P.dma("sp", hc_out[:, c, tt * 512:(tt + 1) * 512], mT[:, c, :], reads=[("mT", c)], writes=[("hc_s", c, tt)])
            if tails is not None:
                tv = tails[c // 4].ap().rearrange("p (c4 m t) -> p c4 m t", c4=4, m=NBLK)
                P.dma("sp", tv[:, c % 4, tt * 4:(tt + 1) * 4, :], mT[:, c, :].rearrange("p (a b) -> p a b", a=4)[:, :, 98:128],
                      reads=[("mT", c)], writes=[("tail", c // 4, c, tt)])


def _k_conv_post(K, hcx_in, wdwT_in, cvec_in, w2, halo=None):
    P = K.P
    K.reset_arena()
    wdw = K.carve([128, 8, 31], F32)
    cv = K.carve([128, 32], F32)
    hx2 = [K.carve([128, 4, 158], F32) for _ in range(2)]
    acc = K.carve([128, 8, 512], F32)
    sqf = K.carve([128, 8, 512], F32)
    mu = K.carve([128, 512], F32)
    var = K.carve([128, 512], F32)
    rstd1 = K.carve([128, 512], F32)
    t2 = [K.carve([128, 512], F32) for _ in range(2)]
    sT = K.carve([128, 8, 512], BF16)
    W = K.carve([128, 8, 1024], BF16)
    mT = K.carve([128, 8, 512], F32)
    sq = K.carve([128, 8, 512], BF16)
    lnv = K.carve([128, 512], F32)
    rstd = K.carve([128, 512], F32)
    onesf = K.carve([128, 128], F32)
    bk = P.bar_keys
    if halo is not None:
        cand2 = [K.carve([128, 4, 4, 30], F32)] * 2
        selw = K.carve([128, 4], F32)
        P.dma("sp", selw, halo["selw"], writes=["selw"], reads=bk)
    P.dma("sp", wdw, wdwT_in, writes=["wdw"], reads=bk)
    P.dma("sp", cv, cvec_in, writes=["cv"], reads=bk)
    P.op("pool", lambda e: e.memset(onesf, 1.0), writes=["onesf"])
    w2, k2_, e2_ = _wsrc(w2)
    wv = w2.rearrange("(c p) e -> p c e", p=128)
    for pc in range(2):
        P.dma(e2_, W[:, :, pc * 512:(pc + 1) * 512], wv[:, :, pc * 512:(pc + 1) * 512], writes=[("w2", pc)], reads=bk + k2_)
    it = 0
    for tt in range(4):
        for c in range(8):
            hx = hx2[it % 2]
            hk = ("hx", it % 2)
            it += 1
            if halo is None:
                P.dma("sp", hx, hcx_in[:, c, tt * 4:(tt + 1) * 4, :], writes=[hk], reads=bk)
            else:
                cand = cand2[(it - 1) % 2]
                ck = ("cand", 0)
                hf = c // 4
                P.dma("sp", hx[:, :, 30:158], halo["hc_s"][:, c, tt * 512:(tt + 1) * 512].rearrange("p (a b) -> p a b", a=4),
                      writes=[hk], reads=bk + [("hc_s", c, tt)])
                agv_ = halo["agt"][hf].ap().rearrange("(j p) (c4 m t) -> p j c4 m t", j=4, c4=4, m=NBLK)
                for jp in range(3):
                    P.dma("sp", cand[:, jp, :, :], agv_[:, jp, c % 4, tt * 4:(tt + 1) * 4, :], writes=[ck], reads=bk + [("agt", hf)])
                if tt == 0:
                    P.op("pool", lambda e, cand=cand: e.memset(cand[:, 3, 0:1, :], 0.0), writes=[ck])
                    P.dma("sp", cand[:, 3, 1:4, :], agv_[:, 3, c % 4, 0:3, :], writes=[ck], reads=bk + [("agt", hf)])
                else:
                    P.dma("sp", cand[:, 3, :, :], agv_[:, 3, c % 4, tt * 4 - 1:tt * 4 + 3, :], writes=[ck], reads=bk + [("agt", hf)])
                P.op("dve", lambda e, hx=hx, cand=cand: e.tensor_scalar(out=hx[:, :, 0:30], in0=cand[:, 0, :, :], scalar1=selw[:, 0:1], scalar2=None, op0=ALU.mult),
                     reads=[ck, "selw"], writes=[hk])
                for jp in range(1, 4):
                    P.op("dve", lambda e, hx=hx, cand=cand, jp=jp: e.scalar_tensor_tensor(out=hx[:, :, 0:30], in0=cand[:, jp, :, :], scalar=selw[:, jp:jp + 1],
                                                                                       in1=hx[:, :, 0:30], op0=ALU.mult, op1=ALU.add),
                         reads=[ck, "selw", hk], writes=[hk])
            av = acc[:, c, :].rearrange("p (a b) -> p a b", a=4)
            P.op("dve", lambda e, hx=hx, c=c, av=av: e.tensor_scalar(out=av, in0=hx[:, :, 0:128], scalar1=wdw[:, c, 0:1], scalar2=cv[:, c:c + 1], op0=ALU.mult, op1=ALU.add),
                 reads=[hk, "wdw", "cv"], writes=[("acc", c)])
            for k in range(1, 31):
                P.op("dve", lambda e, hx=hx, c=c, av=av, k=k: e.scalar_tensor_tensor(out=av, in0=hx[:, :, k:k + 128], scalar=wdw[:, c, k:k + 1], in1=av, op0=ALU.mult, op1=ALU.add),
                     reads=[hk, "wdw", ("acc", c)], writes=[("acc", c)])
        P.op("act", lambda e: e.activation(out=sqf, in_=acc, func=AF.Square), reads=[("acc", c) for c in range(8)], writes=["sqf"])
        for c in range(8):
            P.op("pe", lambda e, c=c: e.matmul(K.bank[0][:, :], onesf, acc[:, c, :], start=(c == 0), stop=(c == 7)),
                 reads=[("acc", c), "onesf"], writes=[("bank", 0)])
        for c in range(8):
            P.op("pe", lambda e, c=c: e.matmul(K.bank[1][:, :], onesf, sqf[:, c, :], start=(c == 0), stop=(c == 7)),
                 reads=["sqf", "onesf"], writes=[("bank", 1)])
        P.op("act", lambda e: e.activation(out=mu, in_=K.bank[0][:, :], func=AF.Copy, scale=1.0 / D), reads=[("bank", 0)], writes=["mu"])
        P.op("dve", lambda e: e.tensor_tensor(out=var, in0=mu, in1=mu, op=ALU.mult), reads=["mu"], writes=["var"])
        P.op("dve", lambda e: e.scalar_tensor_tensor(out=var, in0=K.bank[1][:, :], scalar=1.0 / D, in1=var, op0=ALU.mult, op1=ALU.subtract),
             reads=[("bank", 1), "var"], writes=["var"])
        P.op("act", lambda e: e.activation(out=var, in_=var, func=AF.Ln, bias=K.eps_ap, scale=1.0), reads=["var", "eps"], writes=["var"])
        P.op("act", lambda e: e.activation(out=rstd1, in_=var, func=AF.Exp, scale=-0.5), reads=["var"], writes=["rstd1"])
        for c in range(8):
            a = t2[c % 2]
            P.op("dve", lambda e, c=c, a=a: e.tensor_tensor(out=a, in0=acc[:, c, :], in1=mu, op=ALU.subtract), reads=[("acc", c), "mu"], writes=[("tmp", c % 2)])
            P.op("dve", lambda e, a=a: e.tensor_tensor(out=a, in0=a, in1=rstd1, op=ALU.mult), reads=[("tmp", c % 2), "rstd1"], writes=[("tmp", c % 2)])
            P.op("act", lambda e, c=c, a=a: e.activation(out=sT[:, c, :], in_=a, func=AF.Silu, scale=cv[:, 8 + c:9 + c], bias=cv[:, 16 + c:17 + c]),
                 reads=[("tmp", c % 2), "cv"], writes=[("sT", c)])
        for oc in range(8):
            b = 2 + oc % 2
            for k in range(8):
                P.op("pe", lambda e, k=k, oc=oc, b=b: e.matmul(K.bank[b][:, :], W[:, k, oc * 128:(oc + 1) * 128], sT[:, k, :], start=(k == 0), stop=(k == 7)),
                     reads=[("w2", oc // 4), ("sT", k)], writes=[("bank", b)])
            P.op("act", lambda e, oc=oc, b=b: e.activation(out=mT[:, oc, :], in_=K.bank[b][:, :], func=AF.Identity, bias=cv[:, 24 + oc:25 + oc], scale=1.0),
                 reads=[("bank", b), "cv"], writes=[("mT", oc)])
        K.post_norm_res(tt, mT, [("mT", c) for c in range(8)], 16, sq, lnv, rstd, t2, 7)


TWO_PI_HI = 6.28125
TWO_PI_LO = 0.0019353071795864769
MAGIC = 12582912.0


def _sincos(K, phi, shape, out_sin, out_cos, tA, tB, tag):
    P = K.P
    for off, out, nm in ((0.0, out_sin, "s"), (np.pi / 2, out_cos, "c")):
        P.op("dve", lambda e, off=off: e.tensor_scalar(out=tA, in0=phi, scalar1=1.0 / (2 * np.pi), scalar2=off / (2 * np.pi), op0=ALU.mult, op1=ALU.add),
             reads=[("phi", tag)], writes=[("tA", tag)])
        P.op("dve", lambda e: e.tensor_scalar(out=tA, in0=tA, scalar1=MAGIC, scalar2=None, op0=ALU.add), reads=[("tA", tag)], writes=[("tA", tag)])
        P.op("dve", lambda e: e.tensor_scalar(out=tA, in0=tA, scalar1=-MAGIC, scalar2=None, op0=ALU.add), reads=[("tA", tag)], writes=[("tA", tag)])
        P.op("dve", lambda e: e.scalar_tensor_tensor(out=tB, in0=tA, scalar=-TWO_PI_HI, in1=phi, op0=ALU.mult, op1=ALU.add),
             reads=[("tA", tag), ("phi", tag)], writes=[("tB", tag)])
        P.op("dve", lambda e: e.scalar_tensor_tensor(out=tB, in0=tA, scalar=-TWO_PI_LO, in1=tB, op0=ALU.mult, op1=ALU.add),
             reads=[("tA", tag), ("tB", tag)], writes=[("tB", tag)])
        if off != 0.0:
            P.op("dve", lambda e, off=off: e.tensor_scalar(out=tB, in0=tB, scalar1=float(off), scalar2=None, op0=ALU.add), reads=[("tB", tag)], writes=[("tB", tag)])
        P.op("dve", lambda e: e.tensor_scalar(out=tB, in0=tB, scalar1=3.1415925, scalar2=-3.1415925, op0=ALU.min, op1=ALU.max), reads=[("tB", tag)], writes=[("tB", tag)])
        P.op("act", lambda e, out=out: e.activation(out=out, in_=tB, func=AF.Sin), reads=[("tB", tag)], writes=[(nm, tag)])


def build_C():
    K = Kern(need_h=False, arena=36864)
    P = K.P
    NT = B * S
    u_c = P.dram_in("u_c", [128, NT])
    Bw_in = P.dram_in("Bw", [128, 8, 256])
    lamrep = P.dram_in("lamrep", [128, 3, 512])
    lamst = P.dram_in("lamst", [64, 3, 8])
    jcol_in = P.dram_in("jcol", [128, 1])
    trow_in = P.dram_in("trow", [64, 128])
    CT_in = P.dram_in("CT", [64, 8, 32])
    y_out = P.dram_out("y_out", [NT, 128])
    K.setup_eps()
    K.reset_arena()
    cv = K.carve
    Bw = cv([128, 8, 256], F32)
    lr = cv([128, 3, 512], F32)
    ls = cv([64, 3, 8], F32)
    jcol = cv([128, 1], F32)
    trow = cv([64, 128], F32)
    CT = cv([64, 8, 32], F32)
    Ltri = cv([128, 128], F32)
    onesq = cv([128, 128], F32)
    bk = P.bar_keys
    for dst, src, k in ((Bw, Bw_in, "Bw"), (lr, lamrep, "lr"), (ls, lamst, "ls"), (jcol, jcol_in, "jcol"), (trow, trow_in, "trow"), (CT, CT_in, "CT")):
        P.dma("sp", dst, src, writes=[k], reads=bk)
    P.op("pool", lambda e: e.memset(onesq, 1.0), writes=["onesq"])
    P.op("pool", lambda e: e.affine_select(out=Ltri, in_=onesq, pattern=[[1, 128]], compare_op=ALU.is_ge, fill=0.0, base=0, channel_multiplier=-1),
         reads=["onesq"], writes=["Ltri"])
    P.op("dve", lambda e: e.tensor_scalar(out=CT[:, :, 16:32], in0=CT[:, :, 16:32], scalar1=-1.0, scalar2=None, op0=ALU.mult), reads=["CT"], writes=["CT"])

    W512 = [128, 512]
    dt_t = cv(W512, F32); lrdt = cv(W512, F32); th = cv(W512, F32)
    phi = cv(W512, F32); rho = cv(W512, F32); tA = cv(W512, F32); tB = cv(W512, F32)
    sn = cv(W512, F32); cs = cv(W512, F32); mg = cv(W512, F32)
    ar = cv(W512, F32); ai = cv(W512, F32); er = cv(W512, F32); ei = cv(W512, F32); den = cv(W512, F32)
    q1 = cv(W512, F32); q2 = cv(W512, F32)
    TA = cv([128, 8, 128], F32); TB = cv([128, 8, 128], F32)
    T2r = cv([64, 8, 128], F32); T2i = cv([64, 8, 128], F32)

    def dve(fn, r, w):
        P.op("dve", fn, reads=r, writes=w)

    P.op("act", lambda e: e.activation(out=dt_t, in_=lr[:, 2, :], func=AF.Exp), reads=["lr"], writes=["dt_t"])
    dve(lambda e: e.tensor_tensor(out=lrdt, in0=lr[:, 0, :], in1=dt_t, op=ALU.mult), ["lr", "dt_t"], ["lrdt"])
    dve(lambda e: e.tensor_tensor(out=th, in0=lr[:, 1, :], in1=dt_t, op=ALU.mult), ["lr", "dt_t"], ["th"])
    dve(lambda e: e.tensor_copy(out=phi, in_=th), ["th"], [("phi", "t")])
    _sincos(K, phi, W512, sn, cs, tA, tB, "t")
    P.op("act", lambda e: e.activation(out=mg, in_=lrdt, func=AF.Exp), reads=["lrdt"], writes=["mg"])
    dve(lambda e: e.tensor_tensor(out=ar, in0=mg, in1=cs, op=ALU.mult), ["mg", ("c", "t")], ["ar"])
    dve(lambda e: e.tensor_tensor(out=ai, in0=mg, in1=sn, op=ALU.mult), ["mg", ("s", "t")], ["ai"])
    dve(lambda e: e.tensor_tensor(out=den, in0=lr[:, 0, :], in1=lr[:, 0, :], op=ALU.mult), ["lr"], ["den"])
    dve(lambda e: e.tensor_tensor(out=q1, in0=lr[:, 1, :], in1=lr[:, 1, :], op=ALU.mult), ["lr"], ["q1"])
    dve(lambda e: e.tensor_tensor(out=den, in0=den, in1=q1, op=ALU.add), ["den", "q1"], ["den"])
    dve(lambda e: e.tensor_scalar(out=q1, in0=ar, scalar1=-1.0, scalar2=None, op0=ALU.add), ["ar"], ["q1"])
    dve(lambda e: e.tensor_tensor(out=er, in0=q1, in1=lr[:, 0, :], op=ALU.mult), ["q1", "lr"], ["er"])
    dve(lambda e: e.tensor_tensor(out=q2, in0=ai, in1=lr[:, 1, :], op=ALU.mult), ["ai", "lr"], ["q2"])
    dve(lambda e: e.tensor_tensor(out=er, in0=er, in1=q2, op=ALU.add), ["er", "q2"], ["er"])
    dve(lambda e: e.reciprocal(out=den, in_=den), ["den"], ["den"])
    dve(lambda e: e.tensor_tensor(out=er, in0=er, in1=den, op=ALU.mult), ["er", "den"], ["er"])
    dve(lambda e: e.tensor_tensor(out=ei, in0=ai, in1=lr[:, 0, :], op=ALU.mult), ["ai", "lr"], ["ei"])
    dve(lambda e: e.tensor_tensor(out=q2, in0=q1, in1=lr[:, 1, :], op=ALU.mult), ["q1", "lr"], ["q2"])
    dve(lambda e: e.tensor_tensor(out=ei, in0=ei, in1=q2, op=ALU.subtract), ["ei", "q2"], ["ei"])
    dve(lambda e: e.tensor_tensor(out=ei, in0=ei, in1=den, op=ALU.mult), ["ei", "den"], ["ei"])
    dve(lambda e: e.tensor_scalar(out=phi, in0=th, scalar1=jcol[:, 0:1], scalar2=None, op0=ALU.mult), ["th", "jcol", ("tA", "t"), ("tB", "t")], [("phi", "t")])
    dve(lambda e: e.tensor_scalar(out=rho, in0=lrdt, scalar1=jcol[:, 0:1], scalar2=None, op0=ALU.mult), ["lrdt", "jcol"], ["rho"])
    _sincos(K, phi, W512, sn, cs, tA, tB, "t")
    P.op("act", lambda e: e.activation(out=mg, in_=rho, func=AF.Exp, scale=-1.0), reads=["rho"], writes=["mg"])
    dve(lambda e: e.tensor_tensor(out=ar, in0=mg, in1=cs, op=ALU.mult), ["mg", ("c", "t")], ["ar"])
    dve(lambda e: e.tensor_tensor(out=ai, in0=mg, in1=sn, op=ALU.mult), ["mg", ("s", "t")], ["ai"])
    dve(lambda e: e.tensor_tensor(out=q1, in0=ar, in1=er, op=ALU.mult), ["ar", "er"], ["q1"])
    dve(lambda e: e.tensor_tensor(out=q2, in0=ai, in1=ei, op=ALU.mult), ["ai", "ei"], ["q2"])
    dve(lambda e: e.tensor_tensor(out=q1, in0=q1, in1=q2, op=ALU.add), ["q1", "q2"], ["q1"])
    dve(lambda e: e.tensor_tensor(out=q2, in0=ar, in1=ei, op=ALU.mult), ["ar", "ei", "q1"], ["q2"])
    dve(lambda e: e.tensor_tensor(out=den, in0=ai, in1=er, op=ALU.mult), ["ai", "er"], ["den"])
    dve(lambda e: e.tensor_tensor(out=q2, in0=q2, in1=den, op=ALU.subtract), ["q2", "den"], ["q2"])
    q1v = q1.rearrange("p (g q) -> p g q", g=8)
    q2v = q2.rearrange("p (g q) -> p g q", g=8)
    dve(lambda e: e.tensor_copy(out=TA[:, :, 0:64], in_=q1v), ["q1"], ["TA"])
    dve(lambda e: e.tensor_copy(out=TA[:, :, 64:128], in_=q1v), ["q1"], ["TA"])
    dve(lambda e: e.tensor_copy(out=TB[:, :, 64:128], in_=q2v), ["q2"], ["TB"])
    dve(lambda e: e.tensor_scalar(out=TB[:, :, 0:64], in0=q2v, scalar1=-1.0, scalar2=None, op0=ALU.mult), ["q2"], ["TB"])

    S64 = [64, 8]
    dts = cv(S64, F32); lrdts = cv(S64, F32); ths = cv(S64, F32)
    phs = cv([64, 1024], F32); rhs_ = cv([64, 1024], F32); tAs = cv([64, 1024], F32); tBs = cv([64, 1024], F32)
    sns = cv([64, 1024], F32); css = cv([64, 1024], F32); mgs = cv([64, 1024], F32)
    P.op("act", lambda e: e.activation(out=dts, in_=ls[:, 2, :], func=AF.Exp), reads=["ls"], writes=["dts"])
    dve(lambda e: e.tensor_tensor(out=lrdts, in0=ls[:, 0, :], in1=dts, op=ALU.mult), ["ls", "dts"], ["lrdts"])
    dve(lambda e: e.tensor_tensor(out=ths, in0=ls[:, 1, :], in1=dts, op=ALU.mult), ["ls", "dts"], ["ths"])
    for g in range(8):
        dve(lambda e, g=g: e.tensor_scalar(out=phs[:, g * 128:(g + 1) * 128], in0=trow, scalar1=ths[:, g:g + 1], scalar2=None, op0=ALU.mult),
            ["trow", "ths"], [("phi", "s")])
        dve(lambda e, g=g: e.tensor_scalar(out=rhs_[:, g * 128:(g + 1) * 128], in0=trow, scalar1=lrdts[:, g:g + 1], scalar2=None, op0=ALU.mult),
            ["trow", "lrdts"], ["rhos"])
    _sincos(K, phs, [64, 1024], sns, css, tAs, tBs, "s")
    P.op("act", lambda e: e.activation(out=mgs, in_=rhs_, func=AF.Exp), reads=["rhos"], writes=["mgs"])
    T2rv = T2r.rearrange("p g t -> p (g t)")
    T2iv = T2i.rearrange("p g t -> p (g t)")
    dve(lambda e: e.tensor_tensor(out=T2rv, in0=mgs, in1=css, op=ALU.mult), ["mgs", ("c", "s")], ["T2"])
    dve(lambda e: e.tensor_tensor(out=T2iv, in0=mgs, in1=sns, op=ALU.mult), ["mgs", ("s", "s")], ["T2"])

    ub2 = [cv([128, 512], F32) for _ in range(2)]
    c1 = [cv([128, 128], F32) for _ in range(2)]
    c2 = [cv([128, 128], F32) for _ in range(2)]
    cc = [cv([128, 128], F32) for _ in range(2)]
    aa = [[cv([64, 128], F32) for _ in range(4)] for _ in range(2)]
    xr = [cv([64, 128], F32) for _ in range(2)]
    xi = [cv([64, 128], F32) for _ in range(2)]
    xst = cv([64, 8, 2], F32)
    yst = [cv([128, 128], F32) for _ in range(2)]
    un = 0
    for bb in range(B):
        P.op("pool", lambda e: e.memset(xst, 0.0), writes=[("xst", g) for g in range(8)])
        for blk in range(64):
            gblk = bb * 64 + blk
            if gblk % 4 == 0:
                ub = ub2[(gblk // 4) % 2]
                ukey = ("ub", (gblk // 4) % 2)
                P.dma("sp", ub, u_c[:, gblk * 128:gblk * 128 + 512], writes=[ukey], reads=bk)
            t0 = (gblk % 4) * 128
            ybank = 4 + gblk % 2
            yps = K.bank[ybank]
            for g in range(8):
                s2 = un % 2
                un += 1
                bu, Pb = K.bank[s2], K.bank[2 + s2]
                P.op("pe", lambda e, bu=bu, ub=ub, t0=t0, g=g: e.matmul(bu[:, 0:256], ub[:, t0:t0 + 128], Bw[:, g, :], start=True, stop=True),
                     reads=[ukey, "Bw"], writes=[("bank", s2)])
                dve(lambda e, bu=bu, g=g, s2=s2: e.tensor_tensor(out=c1[s2], in0=bu[:, 0:128], in1=TA[:, g, :], op=ALU.mult), [("bank", s2), "TA"], [("c1", s2)])
                dve(lambda e, bu=bu, g=g, s2=s2: e.tensor_tensor(out=c2[s2], in0=bu[:, 128:256], in1=TB[:, g, :], op=ALU.mult), [("bank", s2), "TB"], [("c2", s2)])
                P.op("pool", lambda e, s2=s2: e.tensor_tensor(out=cc[s2], in0=c1[s2], in1=c2[s2], op=ALU.add), reads=[("c1", s2), ("c2", s2)], writes=[("cc", s2)])
                P.op("pe", lambda e, Pb=Pb, s2=s2: e.matmul(Pb[0:64, 0:128], cc[s2][:, 0:64], Ltri, start=True, stop=True),
                     reads=[("cc", s2), "Ltri"], writes=[("bank", 2 + s2)])
                P.op("pe", lambda e, Pb=Pb, s2=s2: e.matmul(Pb[0:64, 128:256], cc[s2][:, 64:128], Ltri, start=True, stop=True),
                     reads=[("cc", s2), "Ltri"], writes=[("bank", 2 + s2)])
                a1, a2, a3, a4 = aa[s2]
                for (dst, col, xs, tab, nm) in ((a1, 0, 0, T2r, "a1"), (a2, 128, 1, T2i, "a2"), (a3, 0, 0, T2i, "a3"), (a4, 128, 1, T2r, "a4")):
                    dve(lambda e, dst=dst, col=col, xs=xs, tab=tab, Pb=Pb, g=g: e.scalar_tensor_tensor(
                        out=dst, in0=Pb[0:64, col:col + 128], scalar=xst[:, g, xs:xs + 1], in1=tab[:, g, :], op0=ALU.add, op1=ALU.mult),
                        [("bank", 2 + s2), ("xst", g), "T2"], [(nm, s2)])
                P.op("pool", lambda e, s2=s2, a1=a1, a2=a2: e.tensor_tensor(out=xr[s2], in0=a1, in1=a2, op=ALU.subtract),
                     reads=[("a1", s2), ("a2", s2)], writes=[("xr", s2)])
                P.op("pool", lambda e, s2=s2, a3=a3, a4=a4: e.tensor_tensor(out=xi[s2], in0=a3, in1=a4, op=ALU.add),
                     reads=[("a3", s2), ("a4", s2)], writes=[("xi", s2)])
                P.op("act", lambda e, s2=s2, g=g: e.activation(out=xst[:, g, 0:1], in_=xr[s2][:, 127:128], func=AF.Copy), reads=[("xr", s2)], writes=[("xst", g)])
                P.op("act", lambda e, s2=s2, g=g: e.activation(out=xst[:, g, 1:2], in_=xi[s2][:, 127:128], func=AF.Copy), reads=[("xi", s2)], writes=[("xst", g)])
                P.op("pe", lambda e, yps=yps, s2=s2, g=g: e.matmul(yps[:, g * 16:(g + 1) * 16], xr[s2], CT[:, g, 0:16], start=True, stop=False),
                     reads=[("xr", s2), "CT"], writes=[("bank", ybank)])
                P.op("pe", lambda e, yps=yps, s2=s2, g=g: e.matmul(yps[:, g * 16:(g + 1) * 16], xi[s2], CT[:, g, 16:32], start=False, stop=True),
                     reads=[("xi", s2), "CT"], writes=[("bank", ybank)])
            ys = yst[gblk % 2]
            P.op("act", lambda e, ys=ys, yps=yps: e.activation(out=ys, in_=yps[:, 0:128], func=AF.Copy), reads=[("bank", ybank)], writes=[("yst", gblk % 2)])
            P.dma("sp", y_out[gblk * 128:(gblk + 1) * 128, :], ys, reads=[("yst", gblk % 2)])
    return P.finish()


def _mod_inputs(P):
    return (P.dram_in("cT", [128, 8]), P.dram_in("wmod", [D, 6 * D]), P.dram_in("bmodT", [128, 48]), P.dram_in("ngT", [128, 32]))


def build_S1():
    K = Kern()
    P = K.P
    h_in = P.dram_in("h_in", [128, NCH, T])
    mi = _mod_inputs(P)
    u_out = P.dram_out("u_out", [128, NCH, T])
    mv_out = P.dram_out("mv_out", [128, 48])
    K.setup_eps()
    K.load_h(h_in)
    K.mod_stage(*mi)
    K.store_mv(mv_out)
    _k_u_out(K, u_out)
    return P.finish()


def build_D():
    K = Kern()
    P = K.P
    h_in = P.dram_in("h_in", [128, NCH, T])
    mv_in = P.dram_in("mv_in", [128, 48])
    y_in = P.dram_in("y_in", [128, NCH, T])
    u_in = P.dram_in("u_in", [128, NCH, T])
    dT = P.dram_in("dT", [128, 8])
    bT = P.dram_in("bT", [128, 16])
    wglu = P.dram_in("wglu", [D, 2 * D])
    wg = P.dram_in("wg", [D, FF]); wu = P.dram_in("wu", [D, FF]); wd = P.dram_in("wd", [FF, D])
    mi = _mod_inputs(P)
    w1 = P.dram_in("w1", [D, 2 * D])
    b1T = P.dram_in("b1T", [128, 16])
    hc_out = P.dram_out("hc_out", [128, NCH, T])
    h_out = P.dram_out("h_out", [128, NCH, T])
    mv_out = P.dram_out("mv_out", [128, 48])
    K.setup_eps()
    K.load_h(h_in)
    K.load_mv(mv_in)
    _k_s5_post(K, y_in, u_in, dT, bT, wglu)
    K.ffn_stage(wg, wu, wd)
    K.store_h(h_out)
    K.mod_stage(*mi)
    K.store_mv(mv_out)
    _k_conv_pre(K, w1, b1T, hc_out)
    return P.finish()


def build_E():
    K = Kern()
    P = K.P
    h_in = P.dram_in("h_in", [128, NCH, T])
    mv_in = P.dram_in("mv_in", [128, 48])
    hcx = P.dram_in("hcx", [128, NCH, NBLK, 158])
    wdwT = P.dram_in("wdwT", [128, 8, 31])
    cvec = P.dram_in("cvec", [128, 32])
    w2 = P.dram_in("w2", [D, D])
    wg = P.dram_in("wg", [D, FF]); wu = P.dram_in("wu", [D, FF]); wd = P.dram_in("wd", [FF, D])
    mi = _mod_inputs(P)
    wqkv = P.dram_in("wqkv", [D, 3 * D])
    qT_out = P.dram_out("qT_out", [128, NCH, T], BF16)
    kT_out = P.dram_out("kT_out", [128, NCH, T], BF16)
    v_out = P.dram_out("v_out", [T, D], BF16)
    h_out = P.dram_out("h_out", [128, NCH, T])
    mv_out = P.dram_out("mv_out", [128, 48])
    K.setup_eps()
    K.load_h(h_in)
    K.load_mv(mv_in)
    _k_conv_post(K, hcx, wdwT, cvec, w2)
    K.ffn_stage(wg, wu, wd)
    K.store_h(h_out)
    K.mod_stage(*mi)
    K.store_mv(mv_out)
    K.qkv_stage(wqkv, qT_out, kT_out, v_out)
    return P.finish()


def _modmap(inp, layer, b):
    return dict(cT=vec_fm(inp["c"][b], 8), wmod=np.ascontiguousarray(inp["w_mod"][layer]),
                bmodT=vec_fm(inp["b_mod"][layer], 48), ngT=vec_fm(np.ascontiguousarray(inp["norm_g"][layer]).reshape(-1), 32))


def _ffnmap(inp, layer):
    return dict(wg=np.ascontiguousarray(inp["ffn_w_gate"][layer]), wu=np.ascontiguousarray(inp["ffn_w_up"][layer]),
                wd=np.ascontiguousarray(inp["ffn_w_down"][layer]))


def _run_B(ncB, inp, hT, resQ, layer, wo):
    kT_b, v_b = gather_kv(resQ)
    maps = []
    for i in range(NCORES):
        b, j = i // 4, i % 4
        m = dict(h_in=hT[i], mv_in=np.asarray(resQ[i]["mv_out"]), qT_in=np.asarray(resQ[i]["qT_out"]),
                 kT_in=kT_b[b], v_in=v_b[b], masks=make_masks(j), wo=wo)
        m.update(_ffnmap(inp, layer))
        maps.append(m)
    res = run(ncB, maps)
    return [np.asarray(r["h_out"]) for r in res]


_TRACE = {}


def kernel_unfused(**inp):
    inp = {k: np.asarray(v) for k, v in inp.items()}
    f32 = np.float32
    hT = [to_fm(inp["x"][i // 4][core_tokens(i % 4)].astype(f32)) for i in range(NCORES)]
    ncA = build_A()
    maps = []
    for i in range(NCORES):
        m = dict(h_in=hT[i], wqkv=np.ascontiguousarray(inp["sb_w_qkv"][0]))
        m.update(_modmap(inp, 0, i // 4))
        maps.append(m)
    resA = run(ncA, maps)
    ncB = build_B()
    hT = _run_B(ncB, inp, hT, resA, 0, np.ascontiguousarray(inp["sb_w_o"][0]))
    _TRACE["h0"] = hT
    maps = []
    for i in range(NCORES):
        m = dict(h_in=hT[i])
        m.update(_modmap(inp, 1, i // 4))
        maps.append(m)
    resS = run(build_S1(), maps)
    u_full = np.zeros((B, S, D), f32)
    for i in range(NCORES):
        u_full[i // 4, core_tokens(i % 4)] = from_fm(np.asarray(resS[i]["u_out"]))
    bre, bim = inp["s5_b_re"][0], inp["s5_b_im"][0]
    cre, cim = inp["s5_c_re"][0], inp["s5_c_im"][0]
    lre, lim, ldt = inp["s5_lam_re"][0], inp["s5_lam_im"][0], inp["s5_log_dt"][0]
    maps = []
    for i in range(NCORES):
        u_c = np.ascontiguousarray(u_full[:, :, i * 128:(i + 1) * 128].reshape(B * S, 128).T)
        Bw = np.zeros((128, 8, 256), f32)
        CT = np.zeros((64, 8, 32), f32)
        lamrep = np.zeros((128, 3, 512), f32)
        lamst = np.zeros((64, 3, 8), f32)
        for gl in range(8):
            g = i * 8 + gl
            rows = slice(gl * 16, (gl + 1) * 16)
            Bw[rows, gl, 0:64] = bre[g].T
            Bw[rows, gl, 64:128] = bim[g].T
            Bw[rows, gl, 128:192] = bim[g].T
            Bw[rows, gl, 192:256] = bre[g].T
            CT[:, gl, 0:16] = cre[g].T
            CT[:, gl, 16:32] = cim[g].T
            lamrep[:, 0, gl * 64:(gl + 1) * 64] = lre[g][None, :]
            lamrep[:, 1, gl * 64:(gl + 1) * 64] = lim[g][None, :]
            lamrep[:, 2, gl * 64:(gl + 1) * 64] = ldt[g]
            lamst[:, 0, gl] = lre[g]
            lamst[:, 1, gl] = lim[g]
            lamst[:, 2, gl] = ldt[g]
        maps.append(dict(u_c=u_c, Bw=Bw, lamrep=lamrep, lamst=lamst,
                         jcol=np.arange(1, 129, dtype=f32)[:, None].copy(),
                         trow=np.tile(np.arange(1, 129, dtype=f32)[None, :], (64, 1)), CT=CT))
    resC = run(build_C(), maps)
    _TRACE["u1"] = u_full
    y_full = np.zeros((B, S, D), f32)
    for i in range(NCORES):
        y_full[:, :, i * 128:(i + 1) * 128] = np.asarray(resC[i]["y_out"]).reshape(B, S, 128)
    maps = []
    for i in range(NCORES):
        b, j = i // 4, i % 4
        tok = core_tokens(j)
        m = dict(h_in=hT[i], mv_in=np.asarray(resS[i]["mv_out"]), y_in=to_fm(y_full[b][tok]), u_in=np.asarray(resS[i]["u_out"]),
                 dT=vec_fm(inp["s5_d"][0], 8), bT=vec_fm(inp["s5_b_glu"][0], 16), wglu=np.ascontiguousarray(inp["s5_w_glu"][0]),
                 w1=np.ascontiguousarray(inp["cv_w_pw1"][0]), b1T=vec_fm(inp["cv_b_pw1"][0], 16))
        m.update(_ffnmap(inp, 1))
        m.update(_modmap(inp, 2, b))
        maps.append(m)
    resD = run(build_D(), maps)
    hT = [np.asarray(r["h_out"]) for r in resD]
    _TRACE["h1"] = hT
    _TRACE["y1"] = y_full
    hcp = np.zeros((B, 128, NCH, S + 30), f32)
    for i in range(NCORES):
        b, j = i // 4, i % 4
        hcp[b][:, :, 30 + core_tokens(j)] = np.asarray(resD[i]["hc_out"])
    wdwT = np.ascontiguousarray(inp["cv_w_dw"][0].T.reshape(NCH, 128, 31).transpose(1, 0, 2))
    cvec = np.concatenate([vec_fm(inp[k][0], 8) for k in ("cv_b_dw", "cv_ln_g", "cv_ln_b", "cv_b_pw2")], axis=1)
    maps = []
    for i in range(NCORES):
        b, j = i // 4, i % 4
        hcx = np.zeros((128, NCH, NBLK, 158), f32)
        for mb in range(NBLK):
            g0 = (4 * mb + j) * 128
            hcx[:, :, mb, :] = hcp[b][:, :, g0:g0 + 158]
        m = dict(h_in=hT[i], mv_in=np.asarray(resD[i]["mv_out"]), hcx=hcx, wdwT=wdwT, cvec=np.ascontiguousarray(cvec),
                 w2=np.ascontiguousarray(inp["cv_w_pw2"][0]), wqkv=np.ascontiguousarray(inp["sb_w_qkv"][1]))
        m.update(_ffnmap(inp, 2))
        m.update(_modmap(inp, 3, b))
        maps.append(m)
    resE = run(build_E(), maps)
    hT = [np.asarray(r["h_out"]) for r in resE]
    _TRACE["h2"] = hT
    hT = _run_B(ncB, inp, hT, resE, 3, np.ascontiguousarray(inp["sb_w_o"][1]))
    out = np.zeros((B, S, D), f32)
    for i in range(NCORES):
        out[i // 4, core_tokens(i % 4)] = from_fm(hT[i])
    return out


GROUPS4 = [[0, 1, 2, 3], [4, 5, 6, 7]]


def _sb_layer(K, tag, wqkv, wo, masks_in):
    P = K.P
    q_s = P.dram_scratch(f"q_s{tag}", [128, NCH, T], BF16)
    k_s = [P.dram_scratch(f"k_s{tag}_{c}", [128, T], BF16) for c in range(NCH)]
    v_s = [P.dram_scratch(f"v_s{tag}_{c}", [T, 128], BF16) for c in range(NCH)]
    agk = [P.dram_scratch(f"agk{tag}_{c}", [4 * 128, T], BF16) for c in range(NCH)]
    agv = [P.dram_scratch(f"agv{tag}_{c}", [4 * T, 128], BF16) for c in range(NCH)]
    oT_s = P.dram_scratch(f"oT_s{tag}", [128, NCH, T], BF16)

    class KOut:
        def __getitem__(self, idx):
            _, c, sl = idx
            return k_s[c].ap()[:, sl]

    K.qkv_stage(wqkv, q_s.ap(), KOut(), lambda c: v_s[c].ap())
    q_keys = [("qkv_s", "qk", oc, tt) for oc in range(8) for tt in range(4)]
    for c in range(NCH):
        P.cc("AllGather", GROUPS4, k_s[c], agk[c], reads=[("qkv_s", "qk", 8 + c, tt) for tt in range(4)], writes=[("agk", tag, c)])
        P.cc("AllGather", GROUPS4, v_s[c], agv[c],
             reads=[("qkv_s", "v", tt, tb, c // 4, c % 4) for tt in range(4) for tb in range(4)], writes=[("agv", tag, c)])
    K.attn_stage(q_s.ap(), None, None, masks_in, oT_s.ap(),
                 k_src=lambda c: agk[c].ap().rearrange("(j p) (m s) -> p m j s", j=4, m=NBLK),
                 v_src=lambda c: agv[c].ap().rearrange("(j m s) d -> s m j d", j=4, m=NBLK),
                 src_keys=[("agk", tag, c) for c in range(NCH)] + [("agv", tag, c) for c in range(NCH)] + q_keys)
    K.oproj_stage(wo, oT_s.ap())


def build_F(nl=4):
    K = Kern()
    P = K.P
    h_in = P.dram_in("h_in", [128, NCH, T])
    h_out = P.dram_out("h_out", [128, NCH, T])
    masks = P.dram_in("masks", [128, 4, 128])
    cT = P.dram_in("cT", [128, 8])
    wmod = [P.dram_in(f"wmod{l}", [D, 6 * D]) for l in range(nl)]
    bmodT = [P.dram_in(f"bmodT{l}", [128, 48]) for l in range(nl)]
    ngT = [P.dram_in(f"ngT{l}", [128, 32]) for l in range(nl)]
    wg = [P.dram_in(f"wg{l}", [D, FF]) for l in range(nl)]
    wu = [P.dram_in(f"wu{l}", [D, FF]) for l in range(nl)]
    wd = [P.dram_in(f"wd{l}", [FF, D]) for l in range(nl)]
    wqkv0 = P.dram_in("wqkv0", [D, 3 * D]); wo0 = P.dram_in("wo0", [D, D])
    K.setup_eps()
    K.load_h(h_in)
    wqkv0 = K.wbf("wqkv0", wqkv0, D, 3 * D)
    wo0 = K.wbf("wo0", wo0, D, D)
    wg[0] = K.wbf("wg0", wg[0], D, FF); wu[0] = K.wbf("wu0", wu[0], D, FF); wd[0] = K.wbf("wd0", wd[0], FF, D)
    K.F_pre = {}
    if nl > 1:
        _fused_declare(K, nl, wg, wu, wd)
    K.mvs = [P.sb(f"mvs{l}", [128, 48], F32) for l in range(nl)]
    for l in range(nl):
        K.mod_stage(cT, wmod[l], bmodT[l], ngT[l], target=K.mvs[l], tkey=("mvs", l))
    K.use_mod(0)
    _sb_layer(K, "0", wqkv0, wo0, masks)
    K.ffn_stage(wg[0], wu[0], wd[0])
    if nl > 1:
        _fused_rest(K, nl, cT, wmod, bmodT, ngT, wg, wu, wd, masks)
    K.store_h(h_out)
    return P.finish()


def _s5_tables(K, ch, bufs, ins):
    P = K.P
    (lr, ls, jcol, trow, dt_t, lrdt, th, phi, rho, tA, tB, sn, cs, mg, ar, ai, er, ei, den, q1, q2, TA, TB, T2r, T2i,
     dts, lrdts, ths, phs, rhs_, tAs, tBs, sns, css, mgs) = bufs
    bk = P.bar_keys
    P.dma("sp", lr, ins["lamrep"][ch], writes=["lr"], reads=bk)
    P.dma("sp", ls, ins["lamst"][ch], writes=["ls"], reads=bk)
    W512 = [128, 512]

    def dve(fn, r, w):
        P.op("dve", fn, reads=r, writes=w)

    P.op("act", lambda e: e.activation(out=dt_t, in_=lr[:, 2, :], func=AF.Exp), reads=["lr"], writes=["dt_t"])
    dve(lambda e: e.tensor_tensor(out=lrdt, in0=lr[:, 0, :], in1=dt_t, op=ALU.mult), ["lr", "dt_t"], ["lrdt"])
    dve(lambda e: e.tensor_tensor(out=th, in0=lr[:, 1, :], in1=dt_t, op=ALU.mult), ["lr", "dt_t"], ["th"])
    dve(lambda e: e.tensor_copy(out=phi, in_=th), ["th"], [("phi", "t")])
    _sincos(K, phi, W512, sn, cs, tA, tB, "t")
    P.op("act", lambda e: e.activation(out=mg, in_=lrdt, func=AF.Exp), reads=["lrdt"], writes=["mg"])
    dve(lambda e: e.tensor_tensor(out=ar, in0=mg, in1=cs, op=ALU.mult), ["mg", ("c", "t")], ["ar"])
    dve(lambda e: e.tensor_tensor(out=ai, in0=mg, in1=sn, op=ALU.mult), ["mg", ("s", "t")], ["ai"])
    dve(lambda e: e.tensor_tensor(out=den, in0=lr[:, 0, :], in1=lr[:, 0, :], op=ALU.mult), ["lr"], ["den"])
    dve(lambda e: e.tensor_tensor(out=q1, in0=lr[:, 1, :], in1=lr[:, 1, :], op=ALU.mult), ["lr"], ["q1"])
    dve(lambda e: e.tensor_tensor(out=den, in0=den, in1=q1, op=ALU.add), ["den", "q1"], ["den"])
    dve(lambda e: e.tensor_scalar(out=q1, in0=ar, scalar1=-1.0, scalar2=None, op0=ALU.add), ["ar"], ["q1"])
    dve(lambda e: e.tensor_tensor(out=er, in0=q1, in1=lr[:, 0, :], op=ALU.mult), ["q1", "lr"], ["er"])
    dve(lambda e: e.tensor_tensor(out=q2, in0=ai, in1=lr[:, 1, :], op=ALU.mult), ["ai", "lr"], ["q2"])
    dve(lambda e: e.tensor_tensor(out=er, in0=er, in1=q2, op=ALU.add), ["er", "q2"], ["er"])
    dve(lambda e: e.reciprocal(out=den, in_=den), ["den"], ["den"])
    dve(lambda e: e.tensor_tensor(out=er, in0=er, in1=den, op=ALU.mult), ["er", "den"], ["er"])
    dve(lambda e: e.tensor_tensor(out=ei, in0=ai, in1=lr[:, 0, :], op=ALU.mult), ["ai", "lr"], ["ei"])
    dve(lambda e: e.tensor_tensor(out=q2, in0=q1, in1=lr[:, 1, :], op=ALU.mult), ["q1", "lr"], ["q2"])
    dve(lambda e: e.tensor_tensor(out=ei, in0=ei, in1=q2, op=ALU.subtract), ["ei", "q2"], ["ei"])
    dve(lambda e: e.tensor_tensor(out=ei, in0=ei, in1=den, op=ALU.mult), ["ei", "den"], ["ei"])
    dve(lambda e: e.tensor_scalar(out=phi, in0=th, scalar1=jcol[:, 0:1], scalar2=None, op0=ALU.mult), ["th", "jcol", ("tA", "t"), ("tB", "t")], [("phi", "t")])
    dve(lambda e: e.tensor_scalar(out=rho, in0=lrdt, scalar1=jcol[:, 0:1], scalar2=None, op0=ALU.mult), ["lrdt", "jcol"], ["rho"])
    _sincos(K, phi, W512, sn, cs, tA, tB, "t")
    P.op("act", lambda e: e.activation(out=mg, in_=rho, func=AF.Exp, scale=-1.0), reads=["rho"], writes=["mg"])
    dve(lambda e: e.tensor_tensor(out=ar, in0=mg, in1=cs, op=ALU.mult), ["mg", ("c", "t")], ["ar"])
    dve(lambda e: e.tensor_tensor(out=ai, in0=mg, in1=sn, op=ALU.mult), ["mg", ("s", "t")], ["ai"])
    dve(lambda e: e.tensor_tensor(out=q1, in0=ar, in1=er, op=ALU.mult), ["ar", "er"], ["q1"])
    dve(lambda e: e.tensor_tensor(out=q2, in0=ai, in1=ei, op=ALU.mult), ["ai", "ei"], ["q2"])
    dve(lambda e: e.tensor_tensor(out=q1, in0=q1, in1=q2, op=ALU.add), ["q1", "q2"], ["q1"])
    dve(lambda e: e.tensor_tensor(out=q2, in0=ar, in1=ei, op=ALU.mult), ["ar", "ei", "q1"], ["q2"])
    dve(lambda e: e.tensor_tensor(out=den, in0=ai, in1=er, op=ALU.mult), ["ai", "er"], ["den"])
    dve(lambda e: e.tensor_tensor(out=q2, in0=q2, in1=den, op=ALU.subtract), ["q2", "den"], ["q2"])
    q1v = q1.rearrange("p (g q) -> p g q", g=8)
    q2v = q2.rearrange("p (g q) -> p g q", g=8)
    dve(lambda e: e.tensor_copy(out=TA[:, :, 0:64], in_=q1v), ["q1"], ["TA"])
    dve(lambda e: e.tensor_copy(out=TA[:, :, 64:128], in_=q1v), ["q1"], ["TA"])
    dve(lambda e: e.tensor_copy(out=TB[:, :, 64:128], in_=q2v), ["q2"], ["TB"])
    dve(lambda e: e.tensor_scalar(out=TB[:, :, 0:64], in0=q2v, scalar1=-1.0, scalar2=None, op0=ALU.mult), ["q2"], ["TB"])
    P.op("act", lambda e: e.activation(out=dts, in_=ls[:, 2, :], func=AF.Exp), reads=["ls"], writes=["dts"])
    dve(lambda e: e.tensor_tensor(out=lrdts, in0=ls[:, 0, :], in1=dts, op=ALU.mult), ["ls", "dts"], ["lrdts"])
    dve(lambda e: e.tensor_tensor(out=ths, in0=ls[:, 1, :], in1=dts, op=ALU.mult), ["ls", "dts"], ["ths"])
    for g in range(8):
        dve(lambda e, g=g: e.tensor_scalar(out=phs[:, g * 128:(g + 1) * 128], in0=trow, scalar1=ths[:, g:g + 1], scalar2=None, op0=ALU.mult),
            ["trow", "ths", ("tA", "s"), ("tB", "s")], [("phi", "s")])
        dve(lambda e, g=g: e.tensor_scalar(out=rhs_[:, g * 128:(g + 1) * 128], in0=trow, scalar1=lrdts[:, g:g + 1], scalar2=None, op0=ALU.mult),
            ["trow", "lrdts"], ["rhos"])
    _sincos(K, phs, [64, 1024], sns, css, tAs, tBs, "s")
    P.op("act", lambda e: e.activation(out=mgs, in_=rhs_, func=AF.Exp), reads=["rhos"], writes=["mgs"])
    dve(lambda e: e.tensor_tensor(out=T2r.rearrange("p g t -> p (g t)"), in0=mgs, in1=css, op=ALU.mult), ["mgs", ("c", "s")], ["T2"])
    dve(lambda e: e.tensor_tensor(out=T2i.rearrange("p g t -> p (g t)"), in0=mgs, in1=sns, op=ALU.mult), ["mgs", ("s", "s")], ["T2"])


def _s5_fused(K, ins, u_s, y_s, u_sb=None):
    P = K.P
    E_s = P.dram_scratch("s5_E", [64, 2048])
    AGE = P.dram_scratch("s5_AGE", [256, 2048])
    a128_s = P.dram_scratch("s5_a128", [64, 128])
    xm_s = P.dram_scratch("s5_xm", [64, 16 * 128])
    for ps in (1, 2):
        K.reset_arena()
        cv = K.carve
        bk = P.bar_keys
        W512 = [128, 512]
        lr = cv([128, 3, 512], F32); ls = cv([64, 3, 8], F32); jcol = cv([128, 1], F32); trow = cv([64, 128], F32)
        tt_base = K.apos
        tt_ = [cv(W512, F32) for _ in range(17)]
        TA = cv([128, 8, 128], F32); TB = cv([128, 8, 128], F32); T2r = cv([64, 8, 128], F32); T2i = cv([64, 8, 128], F32)
        dts = cv([64, 8], F32); lrdts = cv([64, 8], F32); ths = cv([64, 8], F32)
        ss = [K.arena[0:64, (K.apos - (17 + 2 * 0) * 512 - 4096 - 1024 - 0) * 0:0] for _ in range(0)]
        base_w = tt_base
        ss = [K.arena[0:64, base_w + i * 1024: base_w + (i + 1) * 1024] for i in range(7)]
        bufs = (lr, ls, jcol, trow, *tt_, TA, TB, T2r, T2i, dts, lrdts, ths, *ss)
        Bw = cv([128, 8, 256], BF16 if u_sb is not None else F32); CT = cv([64, 8, 32], F32)
        Ltri = cv([128, 128], F32); onesq = cv([128, 128], F32); ident = cv([128, 128], F32)
        ub2 = [cv([128, 512], BF16 if u_sb is not None else F32) for _ in range(2)]
        c1 = [cv([128, 128], F32) for _ in range(3)]; c2 = [cv([128, 128], F32) for _ in range(3)]; cc = [cv([128, 128], F32) for _ in range(3)]
        P.dma("sp", jcol, ins["jcol"], writes=["jcol"], reads=bk)
        P.dma("sp", trow, ins["trow"], writes=["trow"], reads=bk)
        P.op("pool", lambda e: e.memset(onesq, 1.0), writes=["onesq"])
        P.op("pool", lambda e: e.affine_select(out=Ltri, in_=onesq, pattern=[[1, 128]], compare_op=ALU.is_ge, fill=0.0, base=0, channel_multiplier=-1),
             reads=["onesq"], writes=["Ltri"])
        P.op("pool", lambda e: e.affine_select(out=ident, in_=onesq, pattern=[[1, 128]], compare_op=ALU.is_equal, fill=0.0, base=0, channel_multiplier=-1),
             reads=["onesq"], writes=["ident"])

        def dve(fn, r, w):
            P.op("dve", fn, reads=r, writes=w)

        if ps == 1:
            Ecur = cv([64, 8, 16, 2], F32)
            pt = [cv([64, 16], F32) for _ in range(4)]
            a128 = cv([64, 64, 2], F32)
        else:
            aa = [[cv([64, 128], F32) for _ in range(4)] for _ in range(3)]
            xr = [cv([64, 128], F32) for _ in range(3)]; xi = [cv([64, 128], F32) for _ in range(3)]
            Xc = cv([64, 16, 8, 2], F32)
            yst = [cv([128, 128], F32) for _ in range(2)]
            yT2 = [cv([128, 128], F32) for _ in range(2)]
        un = 0
        for ch in range(8):
            _s5_tables(K, ch, bufs, ins)
            P.dma("pool" if u_sb is not None else "sp", Bw, ins["Bw"][ch], writes=["Bw"], reads=bk)
            if ps == 2:
                P.dma("sp", CT, ins["CT"][ch], writes=["CT"], reads=bk)
                dve(lambda e: e.tensor_scalar(out=CT[:, :, 16:32], in0=CT[:, :, 16:32], scalar1=-1.0, scalar2=None, op0=ALU.mult), ["CT"], ["CT"])
                P.dma("sp", Xc, xm_s.ap().rearrange("p (m q c) -> p m q c", m=16, q=64)[:, :, ch * 8:(ch + 1) * 8, :], writes=["Xc"], reads=bk + ["xm_s"])
            else:
                for g in range(8):
                    P.op("act", lambda e, g=g, ch=ch: e.activation(out=a128[:, ch * 8 + g, 0:1], in_=T2r[:, g, 127:128], func=AF.Copy), reads=["T2"], writes=["a128"])
                    P.op("act", lambda e, g=g, ch=ch: e.activation(out=a128[:, ch * 8 + g, 1:2], in_=T2i[:, g, 127:128], func=AF.Copy), reads=["T2"], writes=["a128"])
            units = []
            for m in range(NBLK):
                for g in range(8):
                    units.append(dict(m=m, g=g, s2=un % 2, s3=un % 3))
                    un += 1

            def s1(u):
                m, g, s2, s3 = u["m"], u["g"], u["s2"], u["s3"]
                ub = ub2[(m // 4) % 2]
                ukey = ("ub", (m // 4) % 2)
                if g == 0 and m % 4 == 0:
                    if u_sb is not None:
                        P.dma("sp", ub, u_sb[:, ch, m * 128:m * 128 + 512], writes=[ukey], reads=bk + [("u_sb", ch, m // 4)])
                    else:
                        P.dma("sp", ub, u_s[:, ch, m * 128:m * 128 + 512], writes=[ukey], reads=bk + [("u_s", ch, m // 4)])
                t0 = (m % 4) * 128
                bu = K.bank[s2]
                P.op("pe", lambda e: e.matmul(bu[:, 0:256], ub[:, t0:t0 + 128], Bw[:, g, :], start=True, stop=True),
                     reads=[ukey, "Bw"], writes=[("bank", s2)])
                dve(lambda e: e.tensor_tensor(out=c1[s3], in0=bu[:, 0:128], in1=TA[:, g, :], op=ALU.mult), [("bank", s2), "TA"], [("c1", s3)])
                dve(lambda e: e.tensor_tensor(out=c2[s3], in0=bu[:, 128:256], in1=TB[:, g, :], op=ALU.mult), [("bank", s2), "TB"], [("c2", s3)])
                P.op("pool", lambda e: e.tensor_tensor(out=cc[s3], in0=c1[s3], in1=c2[s3], op=ALU.add), reads=[("c1", s3), ("c2", s3)], writes=[("cc", s3)])

            def s2f(u):
                m, g, s2, s3 = u["m"], u["g"], u["s2"], u["s3"]
                if ps == 1:
                    col = (g * 16 + m) * 2
                    P.op("pe", lambda e: e.matmul(K.bank[6][0:64, col:col + 1], cc[s3][:, 0:64], onesq[:, 0:1], start=True, stop=True),
                         reads=[("cc", s3), "onesq"], writes=[("bank", 6)])
                    P.op("pe", lambda e: e.matmul(K.bank[6][0:64, col + 1:col + 2], cc[s3][:, 64:128], onesq[:, 0:1], start=True, stop=True),
                         reads=[("cc", s3), "onesq"], writes=[("bank", 6)])
                    return
                Pb = K.bank[2 + s2]
                P.op("pe", lambda e: e.matmul(Pb[0:64, 0:128], cc[s3][:, 0:64], Ltri, start=True, stop=True),
                     reads=[("cc", s3), "Ltri"], writes=[("bank", 2 + s2)])
                P.op("pe", lambda e: e.matmul(Pb[0:64, 128:256], cc[s3][:, 64:128], Ltri, start=True, stop=True),
                     reads=[("cc", s3), "Ltri"], writes=[("bank", 2 + s2)])
                a1, a2, a3, a4 = aa[s3]
                for (dst, col, xs, tab, nm) in ((a1, 0, 0, T2r, "a1"), (a2, 128, 1, T2i, "a2"), (a3, 0, 0, T2i, "a3"), (a4, 128, 1, T2r, "a4")):
                    dve(lambda e, dst=dst, col=col, xs=xs, tab=tab: e.scalar_tensor_tensor(
                        out=dst, in0=Pb[0:64, col:col + 128], scalar=Xc[:, m, g, xs:xs + 1], in1=tab[:, g, :], op0=ALU.add, op1=ALU.mult),
                        [("bank", 2 + s2), "Xc", "T2"], [(nm, s3)])
                P.op("pool", lambda e: e.tensor_tensor(out=xr[s3], in0=a1, in1=a2, op=ALU.subtract),
                     reads=[("a1", s3), ("a2", s3)], writes=[("xr", s3)])
                P.op("pool", lambda e: e.tensor_tensor(out=xi[s3], in0=a3, in1=a4, op=ALU.add),
                     reads=[("a3", s3), ("a4", s3)], writes=[("xi", s3)])

            def s3f(u):
                m, g, s2, s3 = u["m"], u["g"], u["s2"], u["s3"]
                ybank = 4 + m % 2
                yps = K.bank[ybank]
                P.op("pe", lambda e: e.matmul(yps[:, g * 16:(g + 1) * 16], xr[s3], CT[:, g, 0:16], start=True, stop=False),
                     reads=[("xr", s3), "CT"], writes=[("bank", ybank)])
                P.op("pe", lambda e: e.matmul(yps[:, g * 16:(g + 1) * 16], xi[s3], CT[:, g, 16:32], start=False, stop=True),
                     reads=[("xi", s3), "CT"], writes=[("bank", ybank)])
                if g == 7:
                    ys, yT = yst[m % 2], yT2[m % 2]
                    P.op("act", lambda e: e.activation(out=ys, in_=yps[:, 0:128], func=AF.Copy), reads=[("bank", ybank)], writes=[("yst", m % 2)])
                    P.op("pe", lambda e: e.matmul(K.bank[7][:, 0:128], ys, ident, start=True, stop=True),
                         reads=[("yst", m % 2), "ident"], writes=[("bank", 7)])
                    P.op("act", lambda e: e.activation(out=yT, in_=K.bank[7][:, 0:128], func=AF.Copy), reads=[("bank", 7)], writes=[("yT", m % 2)])
                    P.dma("sp", y_s[:, ch, m * 128:(m + 1) * 128], yT, reads=[("yT", m % 2)], writes=[("y_s", ch, m)])

            nU = len(units)
            for k in range(nU + 3):
                if k < nU:
                    s1(units[k])
                if 1 <= k <= nU:
                    s2f(units[k - 1])
                if ps == 2 and k >= 3:
                    s3f(units[k - 3])
            if ps == 1:
                Pv = K.bank[6][0:64, 0:256].rearrange("p (g m c) -> p g m c", g=8, m=16)
                for g in range(8):
                    ar_, ai_ = T2r[:, g, 127:128], T2i[:, g, 127:128]
                    dve(lambda e, g=g, ar_=ar_: e.tensor_scalar(out=pt[0], in0=Pv[:, g, :, 0], scalar1=ar_, scalar2=None, op0=ALU.mult), [("bank", 6), "T2"], ["pt0"])
                    dve(lambda e, g=g, ai_=ai_: e.tensor_scalar(out=pt[1], in0=Pv[:, g, :, 1], scalar1=ai_, scalar2=None, op0=ALU.mult), [("bank", 6), "T2"], ["pt1"])
                    dve(lambda e, g=g, ai_=ai_: e.tensor_scalar(out=pt[2], in0=Pv[:, g, :, 0], scalar1=ai_, scalar2=None, op0=ALU.mult), [("bank", 6), "T2"], ["pt2"])
                    dve(lambda e, g=g, ar_=ar_: e.tensor_scalar(out=pt[3], in0=Pv[:, g, :, 1], scalar1=ar_, scalar2=None, op0=ALU.mult), [("bank", 6), "T2"], ["pt3"])
                    dve(lambda e, g=g: e.tensor_tensor(out=Ecur[:, g, :, 0], in0=pt[0], in1=pt[1], op=ALU.subtract), ["pt0", "pt1"], ["Ecur"])
                    dve(lambda e, g=g: e.tensor_tensor(out=Ecur[:, g, :, 1], in0=pt[2], in1=pt[3], op=ALU.add), ["pt2", "pt3"], ["Ecur"])
                P.dma("sp", E_s.ap()[:, ch * 256:(ch + 1) * 256], Ecur.rearrange("p g m c -> p (g m c)"), reads=["Ecur"], writes=[("E_s", ch)])
        if ps == 1:
            P.dma("sp", a128_s.ap(), a128.rearrange("p q c -> p (q c)"), reads=["a128"], writes=["a128_s"])
            P.cc("AllGather", GROUPS4, E_s, AGE, reads=[("E_s", ch) for ch in range(8)], writes=["AGE"])
            K.reset_arena()
            cv = K.carve
            bk = P.bar_keys
            Eall = cv([64, 4, 8, 8, 16, 2], F32) if False else cv([64, 4 * 2048], F32)
            Ev = Eall.rearrange("p (j h g m c) -> p j h g m c", j=4, h=8, g=8, m=16)
            Xall = cv([64, 64, 64, 2], F32)
            A = cv([64, 64, 2], F32)
            t4 = [cv([64, 64], F32) for _ in range(4)]
            sel = cv([64, 4], F32)
            Xm = cv([64, 16, 64, 2], F32)
            P.dma("sp", Eall.rearrange("p (j q) -> p j q", j=4), AGE.ap().rearrange("(j p) q -> p j q", j=4), reads=bk + ["AGE"], writes=["Eall"])
            P.dma("sp", A.rearrange("p q c -> p (q c)"), a128_s.ap(), reads=bk + ["a128_s"], writes=["A"])
            P.dma("sp", sel, ins["sel64"], reads=bk, writes=["sel"])
            P.op("pool", lambda e: e.memset(Xall[:, 0, :, :], 0.0), writes=[("X", 0)])
            Ar, Ai = A[:, :, 0], A[:, :, 1]
            for gb in range(63):
                m_, j_ = gb // 4, gb % 4
                Er = Ev[:, j_, :, :, m_, 0]
                Ei = Ev[:, j_, :, :, m_, 1]
                Xr, Xi = Xall[:, gb, :, 0], Xall[:, gb, :, 1]
                Nr = Xall[:, gb + 1, :, 0].rearrange("p (h g) -> p h g", h=8)
                Ni = Xall[:, gb + 1, :, 1].rearrange("p (h g) -> p h g", h=8)
                dve(lambda e, Xr=Xr: e.tensor_tensor(out=t4[0], in0=Xr, in1=Ar, op=ALU.mult), [("X", gb), "A"], ["t40"])
                dve(lambda e, Xi=Xi: e.tensor_tensor(out=t4[1], in0=Xi, in1=Ai, op=ALU.mult), [("X", gb), "A"], ["t41"])
                dve(lambda e, Xr=Xr: e.tensor_tensor(out=t4[2], in0=Xr, in1=Ai, op=ALU.mult), [("X", gb), "A"], ["t42"])
                dve(lambda e, Xi=Xi: e.tensor_tensor(out=t4[3], in0=Xi, in1=Ar, op=ALU.mult), [("X", gb), "A"], ["t43"])
                P.op("pool", lambda e: e.tensor_tensor(out=t4[0], in0=t4[0], in1=t4[1], op=ALU.subtract), reads=["t40", "t41"], writes=["t40"])
                P.op("pool", lambda e: e.tensor_tensor(out=t4[2], in0=t4[2], in1=t4[3], op=ALU.add), reads=["t42", "t43"], writes=["t42"])
                P.op("pool", lambda e, Nr=Nr, Er=Er: e.tensor_tensor(out=Nr, in0=t4[0].rearrange("p (h g) -> p h g", h=8), in1=Er, op=ALU.add),
                     reads=["t40", "Eall"], writes=[("X", gb + 1)])
                P.op("pool", lambda e, Ni=Ni, Ei=Ei: e.tensor_tensor(out=Ni, in0=t4[2].rearrange("p (h g) -> p h g", h=8), in1=Ei, op=ALU.add),
                     reads=["t42", "Eall"], writes=[("X", gb + 1)])
            Xv = Xall.rearrange("p (m j) q c -> p m j (q c)", j=4)
            Xmv = Xm.rearrange("p m q c -> p m (q c)")
            allx = [("X", gb) for gb in range(64)]
            dve(lambda e: e.tensor_scalar(out=Xmv, in0=Xv[:, :, 0, :], scalar1=sel[:, 0:1], scalar2=None, op0=ALU.mult), allx + ["sel"], ["Xm"])
            for j_ in range(1, 4):
                dve(lambda e, j_=j_: e.scalar_tensor_tensor(out=Xmv, in0=Xv[:, :, j_, :], scalar=sel[:, j_:j_ + 1], in1=Xmv, op0=ALU.mult, op1=ALU.add),
                    allx + ["sel", "Xm"], ["Xm"])
            P.dma("sp", xm_s.ap(), Xm.rearrange("p m q c -> p (m q c)"), reads=["Xm"], writes=["xm_s"])


def _fused_declare(K, nl, wg, wu, wd):
    P = K.P
    F = K.F_pre
    F["wglu"] = K.wbf("wglu", P.dram_in("wglu", [D, 2 * D]), D, 2 * D)
    wg[1] = K.wbf("wg1", wg[1], D, FF); wu[1] = K.wbf("wu1", wu[1], D, FF); wd[1] = K.wbf("wd1", wd[1], FF, D)
    if nl > 2:
        F["w1"] = K.wbf("w1", P.dram_in("w1", [D, 2 * D]), D, 2 * D)
        F["w2"] = K.wbf("w2", P.dram_in("w2", [D, D]), D, D)
        wg[2] = K.wbf("wg2", wg[2], D, FF); wu[2] = K.wbf("wu2", wu[2], D, FF); wd[2] = K.wbf("wd2", wd[2], FF, D)
    if nl > 3:
        F["wqkv3"] = K.wbf("wqkv3", P.dram_in("wqkv3", [D, 3 * D]), D, 3 * D)
        F["wo3"] = K.wbf("wo3", P.dram_in("wo3", [D, D]), D, D)
        wg[3] = K.wbf("wg3", wg[3], D, FF); wu[3] = K.wbf("wu3", wu[3], D, FF); wd[3] = K.wbf("wd3", wd[3], FF, D)


def _fused_rest(K, nl, cT, wmod, bmodT, ngT, wg, wu, wd, masks):
    P = K.P
    F = K.F_pre
    ins = dict(lamrep=P.dram_in("lamrep", [8, 128, 3, 512]), lamst=P.dram_in("lamst", [8, 64, 3, 8]),
               jcol=P.dram_in("jcol", [128, 1]), trow=P.dram_in("trow", [64, 128]),
               Bw=P.dram_in("Bw", [8, 128, 8, 256]), CT=P.dram_in("CT", [8, 64, 8, 32]), sel64=P.dram_in("sel64", [64, 4]))
    dT = P.dram_in("dT", [128, 8]); bT = P.dram_in("bT", [128, 16]); wglu = F["wglu"]
    u_s = P.dram_scratch("u_s", [128, NCH, T])
    y_s = P.dram_scratch("y_s", [128, NCH, T])
    K.use_mod(1)
    u_sb = P.dram_scratch("u_sb", [128, NCH, T], BF16)
    _k_u_out(K, u_s.ap(), u_out_bf=u_sb.ap())
    _s5_fused(K, ins, u_s.ap(), y_s.ap(), u_sb=u_sb.ap())
    _k_s5_post(K, y_s.ap(), u_s.ap(), dT, bT, wglu)
    K.ffn_stage(wg[1], wu[1], wd[1])
    if nl <= 2:
        return
    w1 = F["w1"]; b1T = P.dram_in("b1T", [128, 16])
    wdwT = P.dram_in("wdwT", [128, 8, 31]); cvec = P.dram_in("cvec", [128, 32]); w2 = F["w2"]
    selw = P.dram_in("selw", [128, 4])
    hc_s = P.dram_scratch("hc_s", [128, NCH, T])
    tails = [P.dram_scratch(f"tail_s{i}", [128, 4 * NBLK * 30]) for i in range(2)]
    agt = [P.dram_scratch(f"agt{i}", [512, 4 * NBLK * 30]) for i in range(2)]
    K.use_mod(2)
    _k_conv_pre(K, w1, b1T, hc_s.ap(), tails=tails)
    for hf in range(2):
        P.cc("AllGather", GROUPS4, tails[hf], agt[hf], reads=[("tail", hf, c, tt) for c in range(hf * 4, hf * 4 + 4) for tt in range(4)],
             writes=[("agt", hf)])
    _k_conv_post(K, None, wdwT, cvec, w2, halo=dict(selw=selw, hc_s=hc_s.ap(), agt=agt))
    K.ffn_stage(wg[2], wu[2], wd[2])
    if nl <= 3:
        return
    wqkv3 = F["wqkv3"]; wo3 = F["wo3"]
    K.use_mod(3)
    _sb_layer(K, "3", wqkv3, wo3, masks)
    K.ffn_stage(wg[3], wu[3], wd[3])


def fused_inputs(inp, nl=4):
    f32 = np.float32
    bre, bim = inp["s5_b_re"][0], inp["s5_b_im"][0]
    cre, cim = inp["s5_c_re"][0], inp["s5_c_im"][0]
    lre, lim, ldt = inp["s5_lam_re"][0], inp["s5_lam_im"][0], inp["s5_log_dt"][0]
    Bw = np.zeros((8, 128, 8, 256), f32); CT = np.zeros((8, 64, 8, 32), f32)
    lamrep = np.zeros((8, 128, 3, 512), f32); lamst = np.zeros((8, 64, 3, 8), f32)
    for ch in range(8):
        for gl in range(8):
            g = ch * 8 + gl
            rows = slice(gl * 16, (gl + 1) * 16)
            Bw[ch, rows, gl, 0:64] = bre[g].T; Bw[ch, rows, gl, 64:128] = bim[g].T
            Bw[ch, rows, gl, 128:192] = bim[g].T; Bw[ch, rows, gl, 192:256] = bre[g].T
            CT[ch, :, gl, 0:16] = cre[g].T; CT[ch, :, gl, 16:32] = cim[g].T
            lamrep[ch, :, 0, gl * 64:(gl + 1) * 64] = lre[g][None, :]
            lamrep[ch, :, 1, gl * 64:(gl + 1) * 64] = lim[g][None, :]
            lamrep[ch, :, 2, gl * 64:(gl + 1) * 64] = ldt[g]
            lamst[ch, :, 0, gl] = lre[g]; lamst[ch, :, 1, gl] = lim[g]; lamst[ch, :, 2, gl] = ldt[g]
    wdwT = np.ascontiguousarray(inp["cv_w_dw"][0].T.reshape(NCH, 128, 31).transpose(1, 0, 2))
    cvec = np.ascontiguousarray(np.concatenate([vec_fm(inp[k][0], 8) for k in ("cv_b_dw", "cv_ln_g", "cv_ln_b", "cv_b_pw2")], axis=1))
    maps = []
    for i in range(NCORES):
        b, j = i // 4, i % 4
        m = dict(h_in=to_fm(inp["x"][b][core_tokens(j)].astype(f32)), masks=make_masks(j), cT=vec_fm(inp["c"][b], 8),
                 wqkv0=np.ascontiguousarray(inp["sb_w_qkv"][0]), wo0=np.ascontiguousarray(inp["sb_w_o"][0]))
        for l in range(nl):
            m[f"wmod{l}"] = np.ascontiguousarray(inp["w_mod"][l]); m[f"bmodT{l}"] = vec_fm(inp["b_mod"][l], 48)
            m[f"ngT{l}"] = vec_fm(np.ascontiguousarray(inp["norm_g"][l]).reshape(-1), 32)
            m[f"wg{l}"] = np.ascontiguousarray(inp["ffn_w_gate"][l]); m[f"wu{l}"] = np.ascontiguousarray(inp["ffn_w_up"][l])
            m[f"wd{l}"] = np.ascontiguousarray(inp["ffn_w_down"][l])
        if nl > 1:
            sel = np.zeros((64, 4), f32); sel[:, j] = 1.0
            m.update(lamrep=lamrep, lamst=lamst, jcol=np.arange(1, 129, dtype=f32)[:, None].copy(),
                     trow=np.tile(np.arange(1, 129, dtype=f32)[None, :], (64, 1)), Bw=Bw, CT=CT, sel64=sel,
                     dT=vec_fm(inp["s5_d"][0], 8), bT=vec_fm(inp["s5_b_glu"][0], 16), wglu=np.ascontiguousarray(inp["s5_w_glu"][0]))
        if nl > 2:
            sw = np.zeros((128, 4), f32)
            if j == 0:
                sw[:, 3] = 1.0
            else:
                sw[:, j - 1] = 1.0
            m.update(w1=np.ascontiguousarray(inp["cv_w_pw1"][0]), b1T=vec_fm(inp["cv_b_pw1"][0], 16), wdwT=wdwT, cvec=cvec,
                     w2=np.ascontiguousarray(inp["cv_w_pw2"][0]), selw=sw)
        if nl > 3:
            m.update(wqkv3=np.ascontiguousarray(inp["sb_w_qkv"][1]), wo3=np.ascontiguousarray(inp["sb_w_o"][1]))
        maps.append(m)
    return maps


def kernel(**inp):
    inp = {k: np.asarray(v) for k, v in inp.items()}
    maps = fused_inputs(inp, 4)
    nc = build_F(4)
    res = run(nc, maps)
    out = np.zeros((B, S, D), np.float32)
    for i in range(NCORES):
        out[i // 4, core_tokens(i % 4)] = from_fm(np.asarray(res[i]["h_out"]))
    return out
```
